# Optimizing a Trainium2 kernel written in Bass

```python
import jax, jax.numpy as jnp
from jax import lax
import numpy as np

D_MODEL = 1024
BATCH = 16
SEQ = 2048
DEPTH = 4
DEC_BATCH = 16
DEC_SEQ = 32
PAST_LEN = 4096

CHUNK = 64
N_MIXERS = 2
N_ATTN_LAYERS = (DEPTH + 1) // 2
N_CONV_LAYERS = DEPTH // 2
N_HEADS = 16
N_KV_HEADS = 4
HEAD_DIM = D_MODEL // N_HEADS
GROUP = N_HEADS // N_KV_HEADS
Q_DIM = N_HEADS * HEAD_DIM
KV_DIM = N_KV_HEADS * HEAD_DIM
QKV_DIM = Q_DIM + 2 * KV_DIM
WINDOW = 128
WINDOW_CHUNKS = WINDOW // CHUNK
BAND = WINDOW + CHUNK
CONV_WIDTH = 31
CONV_PAD = CONV_WIDTH - 1
FFN_DIM = 2816
NORM_EPS = 1e-5
NEG_INF = -1e30

kernel_name = 'streaming_swa_sink_conformer_hybrid'


def _rmsnorm(x, g):
    xf = x.astype(jnp.float32)
    y = xf * lax.rsqrt(jnp.mean(xf * xf, axis=-1, keepdims=True) + NORM_EPS)
    return (y * g.astype(jnp.float32)).astype(x.dtype)


def _layernorm(x, g, b):
    xf = x.astype(jnp.float32)
    mu = jnp.mean(xf, axis=-1, keepdims=True)
    xc = xf - mu
    y = xc * lax.rsqrt(jnp.mean(xc * xc, axis=-1, keepdims=True) + NORM_EPS)
    return y * g.astype(jnp.float32) + b.astype(jnp.float32)


def _swiglu(h, w_in, w_out):
    gu = h @ w_in
    g, u = gu[..., :FFN_DIM], gu[..., FFN_DIM:]
    return (jax.nn.silu(g) * u) @ w_out


def _alibi_bias(tq, tk):
    slopes = jnp.asarray(np.array([2.0 ** (-8.0 * (h + 1) / N_HEADS) for h in range(N_HEADS)], dtype=np.float32))
    i = jnp.arange(tq)[:, None]
    j = jnp.arange(tk)[None, :]
    dist = jnp.abs(i - (j - WINDOW)).astype(jnp.float32)
    return (-slopes[:, None, None] * dist).reshape(N_KV_HEADS, GROUP, tq, tk)


def _qkv(h, w_qkv, b_qkv):
    p = h @ w_qkv + b_qkv
    lead = h.shape[:-1]
    q = p[..., :Q_DIM].reshape(*lead, N_KV_HEADS, GROUP, HEAD_DIM)
    k = p[..., Q_DIM:Q_DIM + KV_DIM].reshape(*lead, N_KV_HEADS, HEAD_DIM)
    v = p[..., Q_DIM + KV_DIM:].reshape(*lead, N_KV_HEADS, HEAD_DIM)
    return q, k, v


def _attend(q, k, v, bias, mask, sinks):
    s = jnp.einsum('...qkgd,...skd->...kgqs', q, k).astype(jnp.float32) * (HEAD_DIM ** -0.5) + bias
    if mask is not None:
        s = jnp.where(mask, s, NEG_INF)
    sink = sinks.astype(jnp.float32).reshape(N_KV_HEADS, GROUP, 1, 1)
    m = jnp.maximum(jnp.max(s, axis=-1, keepdims=True), sink)
    p = jnp.exp(s - m)
    denom = jnp.sum(p, axis=-1, keepdims=True) + jnp.exp(sink - m)
    return jnp.einsum('...kgqs,...skd->...qkgd', (p / denom).astype(v.dtype), v)


def _attn_prompt(h, w_qkv, b_qkv, w_o, sinks):
    b, s, _ = h.shape
    nc = s // CHUNK
    q, k, v = _qkv(h, w_qkv, b_qkv)
    pad = ((0, 0), (WINDOW, 0), (0, 0), (0, 0))
    kp = jnp.pad(k, pad).reshape(b, nc + WINDOW_CHUNKS, CHUNK, N_KV_HEADS, HEAD_DIM)
    vp = jnp.pad(v, pad).reshape(b, nc + WINDOW_CHUNKS, CHUNK, N_KV_HEADS, HEAD_DIM)
    kb = jnp.concatenate([kp[:, j:j + nc] for j in range(WINDOW_CHUNKS + 1)], axis=2)
    vb = jnp.concatenate([vp[:, j:j + nc] for j in range(WINDOW_CHUNKS + 1)], axis=2)
    qb = q.reshape(b, nc, CHUNK, N_KV_HEADS, GROUP, HEAD_DIM)
    key_pos = jnp.arange(nc)[:, None] * CHUNK + jnp.arange(BAND)[None, :] - WINDOW
    mask = (key_pos >= 0)[:, None, None, None, :]
    o = _attend(qb, kb, vb, _alibi_bias(CHUNK, BAND), mask, sinks)
    out = o.reshape(b, s, Q_DIM) @ w_o
    return out, k[:, -WINDOW:], v[:, -WINDOW:]


def _attn_sample(h, ck, cv, w_qkv, b_qkv, w_o, sinks):
    b, t, _ = h.shape
    q, k, v = _qkv(h, w_qkv, b_qkv)
    kk = jnp.concatenate([ck.astype(k.dtype), k], axis=1)
    vv = jnp.concatenate([cv.astype(v.dtype), v], axis=1)
    o = _attend(q, kk, vv, _alibi_bias(t, WINDOW + t), None, sinks)
    out = o.reshape(b, t, Q_DIM) @ w_o
    return out, kk[:, -WINDOW:], vv[:, -WINDOW:]


def _conv_module(h, buf, w_in, b_in, w_dw, b_dw, ln_g, ln_b, w_out, b_out):
    ag = h @ w_in + b_in
    u = ag[..., :D_MODEL] * jax.nn.sigmoid(ag[..., D_MODEL:])
    up = jnp.concatenate([buf.astype(u.dtype), u], axis=1)
    y = lax.conv_general_dilated(up, w_dw.astype(up.dtype)[:, None, :], window_strides=(1,), padding='VALID',
                                 dimension_numbers=('NWC', 'WIO', 'NWC'), feature_group_count=D_MODEL) + b_dw
    y = jax.nn.silu(_layernorm(y, ln_g, ln_b)).astype(h.dtype)
    return y @ w_out + b_out, up[:, -CONV_PAD:]


def _trunk(x, cache_k, cache_v, state_conv, norm_ffn1, norm_mix, norm_ffn2, norm_final,
           ffn1_w_in, ffn1_w_out, ffn2_w_in, ffn2_w_out, attn_w_qkv, attn_b_qkv, attn_w_o, attn_sinks,
           conv_w_in, conv_b_in, conv_w_dw, conv_b_dw, conv_ln_g, conv_ln_b, conv_w_out, conv_b_out):
    prompt = cache_k is None
    new_k, new_v, new_conv = [], [], []
    for layer in range(DEPTH):
        x = x + 0.5 * _swiglu(_rmsnorm(x, norm_ffn1[layer]), ffn1_w_in[layer], ffn1_w_out[layer])
        h = _rmsnorm(x, norm_mix[layer])
        if layer % N_MIXERS == 0:
            a = layer // N_MIXERS
            if prompt:
                out, k, v = _attn_prompt(h, attn_w_qkv[a], attn_b_qkv[a], attn_w_o[a], attn_sinks[a])
            else:
                out, k, v = _attn_sample(h, cache_k[a], cache_v[a], attn_w_qkv[a], attn_b_qkv[a], attn_w_o[a], attn_sinks[a])
            new_k.append(k)
            new_v.append(v)
        else:
            c = layer // N_MIXERS
            buf = jnp.zeros((x.shape[0], CONV_PAD, D_MODEL), x.dtype) if prompt else state_conv[c]
            out, st = _conv_module(h, buf, conv_w_in[c], conv_b_in[c], conv_w_dw[c], conv_b_dw[c],
                                   conv_ln_g[c], conv_ln_b[c], conv_w_out[c], conv_b_out[c])
            new_conv.append(st)
        x = x + out
        x = x + 0.5 * _swiglu(_rmsnorm(x, norm_ffn2[layer]), ffn2_w_in[layer], ffn2_w_out[layer])
    y = _rmsnorm(x, norm_final)
    return y, jnp.stack(new_k), jnp.stack(new_v), jnp.stack(new_conv)


def setup_inputs(seed: int = 0) -> dict:
    key = jax.random.key(seed)
    ks = jax.random.split(key, 32)
    f32 = jnp.float32
    nrm = lambda k, shape, scale: (jax.random.normal(k, shape, f32) * scale)
    return {
        'x_prompt': nrm(ks[0], (BATCH, SEQ, D_MODEL), 1.0),
        'x_sample': nrm(ks[1], (DEC_BATCH, DEC_SEQ, D_MODEL), 1.0),
        'cache_k': nrm(ks[2], (N_ATTN_LAYERS, DEC_BATCH, WINDOW, N_KV_HEADS, HEAD_DIM), 1.0),
        'cache_v': nrm(ks[3], (N_ATTN_LAYERS, DEC_BATCH, WINDOW, N_KV_HEADS, HEAD_DIM), 1.0),
        'state_conv': nrm(ks[4], (N_CONV_LAYERS, DEC_BATCH, CONV_PAD, D_MODEL), 1.0),
        'norm_ffn1': 1.0 + nrm(ks[5], (DEPTH, D_MODEL), 0.02),
        'norm_mix': 1.0 + nrm(ks[6], (DEPTH, D_MODEL), 0.02),
        'norm_ffn2': 1.0 + nrm(ks[7], (DEPTH, D_MODEL), 0.02),
        'norm_final': 1.0 + nrm(ks[8], (D_MODEL,), 0.02),
        'ffn1_w_in': nrm(ks[9], (DEPTH, D_MODEL, 2 * FFN_DIM), D_MODEL ** -0.5),
        'ffn1_w_out': nrm(ks[10], (DEPTH, FFN_DIM, D_MODEL), FFN_DIM ** -0.5),
        'ffn2_w_in': nrm(ks[11], (DEPTH, D_MODEL, 2 * FFN_DIM), D_MODEL ** -0.5),
        'ffn2_w_out': nrm(ks[12], (DEPTH, FFN_DIM, D_MODEL), FFN_DIM ** -0.5),
        'attn_w_qkv': nrm(ks[13], (N_ATTN_LAYERS, D_MODEL, QKV_DIM), D_MODEL ** -0.5),
        'attn_b_qkv': nrm(ks[14], (N_ATTN_LAYERS, QKV_DIM), 0.01),
        'attn_w_o': nrm(ks[15], (N_ATTN_LAYERS, Q_DIM, D_MODEL), Q_DIM ** -0.5),
        'attn_sinks': nrm(ks[16], (N_ATTN_LAYERS, N_HEADS), 0.5),
        'conv_w_in': nrm(ks[17], (N_CONV_LAYERS, D_MODEL, 2 * D_MODEL), D_MODEL ** -0.5),
        'conv_b_in': nrm(ks[18], (N_CONV_LAYERS, 2 * D_MODEL), 0.01),
        'conv_w_dw': nrm(ks[19], (N_CONV_LAYERS, CONV_WIDTH, D_MODEL), CONV_WIDTH ** -0.5),
        'conv_b_dw': nrm(ks[20], (N_CONV_LAYERS, D_MODEL), 0.01),
        'conv_ln_g': 1.0 + nrm(ks[21], (N_CONV_LAYERS, D_MODEL), 0.02),
        'conv_ln_b': nrm(ks[22], (N_CONV_LAYERS, D_MODEL), 0.01),
        'conv_w_out': nrm(ks[23], (N_CONV_LAYERS, D_MODEL, D_MODEL), D_MODEL ** -0.5),
        'conv_b_out': nrm(ks[24], (N_CONV_LAYERS, D_MODEL), 0.01),
    }


def reference(x_prompt, x_sample, cache_k, cache_v, state_conv, norm_ffn1, norm_mix, norm_ffn2, norm_final,
              ffn1_w_in, ffn1_w_out, ffn2_w_in, ffn2_w_out, attn_w_qkv, attn_b_qkv, attn_w_o, attn_sinks,
              conv_w_in, conv_b_in, conv_w_dw, conv_b_dw, conv_ln_g, conv_ln_b, conv_w_out, conv_b_out):
    y_prompt, new_k_prompt, new_v_prompt, new_conv_prompt = _trunk(
        x_prompt, None, None, None, norm_ffn1, norm_mix, norm_ffn2, norm_final,
        ffn1_w_in, ffn1_w_out, ffn2_w_in, ffn2_w_out, attn_w_qkv, attn_b_qkv, attn_w_o, attn_sinks,
        conv_w_in, conv_b_in, conv_w_dw, conv_b_dw, conv_ln_g, conv_ln_b, conv_w_out, conv_b_out)
    y_sample, new_k_sample, new_v_sample, new_conv_sample = _trunk(
        x_sample, cache_k, cache_v, state_conv, norm_ffn1, norm_mix, norm_ffn2, norm_final,
        ffn1_w_in, ffn1_w_out, ffn2_w_in, ffn2_w_out, attn_w_qkv, attn_b_qkv, attn_w_o, attn_sinks,
        conv_w_in, conv_b_in, conv_w_dw, conv_b_dw, conv_ln_g, conv_ln_b, conv_w_out, conv_b_out)
    return (y_prompt, y_sample, new_k_prompt, new_v_prompt, new_conv_prompt, new_k_sample, new_v_sample, new_conv_sample)
```

```python
from contextlib import ExitStack
import numpy as np
import concourse.bass as bass
import concourse.mybir as mybir
from concourse.bass_utils import run_bass_kernel_spmd

F32 = mybir.dt.float32
BF16 = mybir.dt.bfloat16
I32 = mybir.dt.int32
AF = mybir.ActivationFunctionType
ALU = mybir.AluOpType
AX = mybir.AxisListType

D = 1024
FF = 2816
NCH = 8
NJ = 22
NH = 16
NKV = 4
HD = 64
CW = 31
CP = 30
WIN = 128
EPS = 1e-5
NEG = -1e30
N_CORES = 8


class Buf:
    __slots__ = ("name", "w", "r")

    def __init__(self, name):
        self.name = name
        self.w = {}
        self.r = {}


class Eng:
    def __init__(self, name, h, sem):
        self.name = name
        self.h = h
        self.sem = sem
        self.count = 0
        self.waited = {}


class DSem:
    def __init__(self, sem):
        self.sem = sem
        self.val = 0


class Sched:
    def __init__(self, nc, stack):
        self.nc = nc
        self.stack = stack
        self.eng = {}
        for name, h in (("pe", nc.tensor), ("act", nc.scalar), ("dve", nc.vector),
                        ("pool", nc.gpsimd), ("sp", nc.sync)):
            sem = stack.enter_context(nc.semaphore("sem_" + name))
            self.eng[name] = Eng(name, h, sem)
        self.n_dsem = 0
        self.nops = 0
        self.nwaits = 0

    def dsem(self):
        self.n_dsem += 1
        return DSem(self.stack.enter_context(self.nc.semaphore("dsem%d" % self.n_dsem)))

    def _need(self, e, reads, writes):
        need = {}
        for b in reads:
            for k, (sem, val) in b.w.items():
                if e.waited.get(k, 0) < val and need.get(k, (None, 0))[1] < val:
                    need[k] = (sem, val)
        for b in writes:
            for d in (b.w, b.r):
                for k, (sem, val) in d.items():
                    if e.waited.get(k, 0) < val and need.get(k, (None, 0))[1] < val:
                        need[k] = (sem, val)
        return need

    def op(self, ename, fn, reads=(), writes=(), dsem=None, after=()):
        e = self.eng[ename]
        need = self._need(e, list(reads) + list(after), writes)
        if ename == "pe":
            need.pop(id(e.sem), None)
        items = list(need.items())
        attach = None
        if items and ename != "pe":
            attach = items.pop()
        for k, (sem, val) in items:
            e.h.wait_ge(sem, val)
            e.waited[k] = val
            self.nwaits += 1
        res = fn(e.h)
        inss = list(res) if isinstance(res, (list, tuple)) else [res]
        if attach is not None:
            k, (sem, val) = attach
            inss[0].wait_op(sem, val, "sem-ge")
            e.waited[k] = val
            self.nwaits += 1
        if dsem is not None:
            for i in inss:
                i.then_inc(dsem.sem, 16)
            dsem.val += 16 * len(inss)
            tok = (dsem.sem, dsem.val)
        else:
            e.count += 1
            inss[-1].then_inc(e.sem, 1)
            tok = (e.sem, e.count)
        key = id(tok[0])
        for b in writes:
            b.w = {key: tok}
            b.r = {}
        for b in reads:
            if b not in writes:
                b.r[key] = tok
        self.nops += 1
        return tok

    def wait_all(self, ename, bufs):
        e = self.eng[ename]
        need = self._need(e, (), bufs)
        for k, (sem, val) in need.items():
            e.h.wait_ge(sem, val)
            e.waited[k] = val


def fence(old_bufs, new_bufs):
    comb = {}
    for b in old_bufs:
        for d in (b.w, b.r):
            for k, (sem, val) in d.items():
                if comb.get(k, (None, 0))[1] < val:
                    comb[k] = (sem, val)
    for b in new_bufs:
        b.w = dict(comb)
        b.r = {}


def build(cfg):
    depth = cfg["depth"]
    seq = cfg["seq"]
    dseq = cfg["dseq"]
    NT = seq // 512
    n_attn = (depth + 1) // 2
    n_conv = depth // 2
    NS = 2 * dseq

    nc = bass.Bass("TRN2", target_bir_lowering=False)

    def din(name, shape):
        return nc.dram_tensor(name, list(shape), F32, kind="ExternalInput").ap()

    def dout(name, shape):
        return nc.dram_tensor(name, list(shape), F32, kind="ExternalOutput").ap()

    def dscr(name, shape):
        return nc.dram_tensor(name, list(shape), BF16, kind="Internal").ap()

    x_prompt = din("x_prompt", (2, seq, D))
    x_sample = din("x_sample", (2, dseq, D))
    cache_k = din("cache_k", (n_attn, 2, WIN, NKV * HD))
    cache_v = din("cache_v", (n_attn, 2, WIN, NKV * HD))
    state_conv = din("state_conv", (max(n_conv, 1), 2, CP, D))
    NVEC = 3 * depth + 1 + max(n_conv, 1) * (2 + 4 + CW)
    vecs = din("vecs", (NVEC * 8, 128))
    qkb = din("qkb", (n_attn * 20, 64))
    kvb = din("kvb", (n_attn, 1, 512))
    sinks = din("sinks", (n_attn, 1, NH))
    alibi = din("alibi", (128, NH * 256))
    ffn_w_in = [din("ffn1_w_in", (depth, D, 2 * FF)), din("ffn2_w_in", (depth, D, 2 * FF))]
    ffn_w_out = [din("ffn1_w_out", (depth, FF, D)), din("ffn2_w_out", (depth, FF, D))]
    attn_w_qkv = din("attn_w_qkv", (n_attn, D, 1536))
    attn_w_o = din("attn_w_o", (n_attn, D, D))
    conv_w_in = din("conv_w_in", (max(n_conv, 1), D, 2 * D))
    conv_w_out = din("conv_w_out", (max(n_conv, 1), D, D))

    y_prompt = dout("y_prompt", (2, seq, D))
    y_sample = dout("y_sample", (2, dseq, D))
    nk_p = dout("new_k_prompt", (n_attn, 2, WIN, 256))
    nv_p = dout("new_v_prompt", (n_attn, 2, WIN, 256))
    ncv_p = dout("new_conv_prompt", (max(n_conv, 1), 2, CP, D))
    nk_s = dout("new_k_sample", (n_attn, 2, WIN, 256))
    nv_s = dout("new_v_sample", (n_attn, 2, WIN, 256))
    ncv_s = dout("new_conv_sample", (max(n_conv, 1), 2, CP, D))

    s_w_in = [dscr("s_ffn1_w_in", (depth, D, 2 * FF)), dscr("s_ffn2_w_in", (depth, D, 2 * FF))]
    s_w_out = [dscr("s_ffn1_w_out", (depth, FF, D)), dscr("s_ffn2_w_out", (depth, FF, D))]
    s_qkv = dscr("s_qkv", (n_attn, D, 1536))
    s_wo = dscr("s_wo", (n_attn, D, D))
    s_cin = dscr("s_cin", (max(n_conv, 1), D, 2 * D))
    s_cout = dscr("s_cout", (max(n_conv, 1), D, D))
    s_diag = dscr("s_diag", (max(n_conv, 1), 128, NCH * CW * 128))

    st = ExitStack()
    with st:
        S = Sched(nc, st)
        st.enter_context(nc.allow_low_precision("bf16 matmul operands, fp32 accumulation"))

        def sb(name, shape, dt):
            return st.enter_context(nc.sbuf_tensor(name, list(shape), dt))

        xT = sb("xT", (128, NCH, 512), F32)
        b_x = [Buf("x%d" % c) for c in range(NCH)]
        hT = sb("hT", (128, NCH, 512), BF16)
        b_h = [Buf("h%d" % c) for c in range(NCH)]
        ringA = [sb("ringA%d" % i, (128, 8192), BF16) for i in range(3)]
        b_rA = [Buf("rA%d" % i) for i in range(3)]
        d_rA = [S.dsem() for _ in range(3)]
        ringB = [sb("ringB%d" % i, (128, 4096), BF16) for i in range(3)]
        b_rB = [Buf("rB%d" % i) for i in range(3)]
        d_rB = [S.dsem() for _ in range(3)]
        rA_i = [0]
        rB_i = [0]
        sg = [sb("sg%d" % i, (128, 512), BF16) for i in range(2)]
        b_sg = [Buf("sg%d" % i) for i in range(2)]
        rs = sb("rs", (128, 512), F32)
        b_rs = Buf("rs")
        ssum = sb("ssum", (128, 512), F32)
        b_ssum = Buf("ssum")
        ident = sb("ident", (128, 128), F32)
        identb = sb("identb", (128, 128), BF16)
        ones = sb("ones", (128, 128), F32)
        b_const = Buf("const")
        b_bias = Buf("biasT")
        b_btmp = Buf("btmp")
        PAR = sb("PAR", (128, NVEC * 8), F32)
        b_par = Buf("par")
        QKB = sb("QKB", (64, n_attn * 20), F32)
        KVB = sb("KVB", (128, n_attn, 512), F32)
        SNK = sb("SNK", (128, n_attn, NH, 1), F32)
        biasT = sb("biasT", (128, NH, 256), F32)
        kcar = [sb("kcar%d" % a, (64, NKV, WIN), BF16) for a in range(n_attn)]
        vcar = [sb("vcar%d" % a, (128, 256), BF16) for a in range(n_attn)]
        ucar = [sb("ucar%d" % c, (128, NCH, CP), BF16) for c in range(max(n_conv, 1))]
        b_kcar = [Buf("kcar%d" % a) for a in range(n_attn)]
        b_vcar = [Buf("vcar%d" % a) for a in range(n_attn)]
        b_ucar = [Buf("ucar%d" % c) for c in range(max(n_conv, 1))]
        small = sb("small", (128, 2, 12), F32)
        b_small = [Buf("small%d" % i) for i in range(2)]
        rtok = sb("rtok", (128, 4), F32)
        b_rtok = Buf("rtok")
        SCW = 16896
        SC = sb("SC", (128, SCW), F32)

        PS = st.enter_context(nc.psum_tensor("PS", [128, 4096], F32))
        banks = [PS[:, i * 512:(i + 1) * 512] for i in range(8)]
        b_bank = [Buf("bank%d" % i) for i in range(8)]

        def scv(off, words, dt=F32, parts=128):
            ap = SC[0:parts, off:off + words]
            return ap.bitcast(BF16) if dt == BF16 else ap

        sq = scv(0, 4096).rearrange("p (c n) -> p c n", c=NCH)
        aT = scv(4096, 5632, BF16).rearrange("p (j n) -> p j n", j=NJ)
        b_sq = [Buf("sq%d" % c) for c in range(NCH)]
        b_aT = [Buf("aT%d" % j) for j in range(NJ)]
        ffn_bufs = b_sq + b_aT
        qT = scv(0, 4096, BF16, 64).rearrange("p (h n) -> p h n", h=NH)
        oT = scv(4096, 4096, BF16, 64).rearrange("p (h n) -> p h n", h=NH)
        kTt = scv(8192, 1280, BF16, 64).rearrange("p (k n) -> p k n", k=NKV)
        Vt = scv(9472, 640, BF16).rearrange("p (b n) -> p b n", b=5)
        Sp = [scv(10112 + 1040 * i, 1040).rearrange("p (h n) -> p h n", h=4) for i in range(2)]
        Pn = [scv(12192 + 512 * i, 512, BF16).rearrange("p (h n) -> p h n", h=4) for i in range(2)]
        PTs = [scv(13216 + 512 * i, 512, BF16).rearrange("p (h n) -> p h n", h=4) for i in range(2)]
        kvtok = [scv(14240 + 512 * i, 512) for i in range(2)]
        kstage = scv(15264, 256)
        vstage = scv(15520, 256)
        junk = scv(15776, 128)
        b_qT = [Buf("qT%d" % h) for h in range(NH)]
        b_oT = [Buf("oT%d" % h) for h in range(NH)]
        b_kT = Buf("kT")
        b_V = Buf("V")
        b_Sp = [Buf("Sp%d" % i) for i in range(2)]
        b_Pn = [Buf("Pn%d" % i) for i in range(2)]
        b_PTs = [Buf("PTs%d" % i) for i in range(2)]
        b_kvtok = [Buf("kvtok%d" % i) for i in range(2)]
        b_kstage = Buf("kstage")
        b_vstage = Buf("vstage")
        b_junk = Buf("junk")
        attn_bufs = b_qT + b_oT + [b_kT, b_V] + b_Sp + b_Pn + b_PTs + b_kvtok + [b_kstage, b_vstage, b_junk]
        ub = scv(0, 2176, BF16).rearrange("p (c n) -> p c n", c=NCH)
        ust = scv(2176, 480).rearrange("p (c s t) -> p c s t", c=NCH, s=2)
        yT = scv(2688, 4096).rearrange("p (c n) -> p c n", c=NCH)
        zT = scv(6784, 2048, BF16).rearrange("p (c n) -> p c n", c=NCH)
        cst = [scv(8832 + 512 * i, 512) for i in range(4)]
        ysq = [scv(10880 + 512 * i, 512) for i in range(2)]
        sst = scv(11904, 1024)
        b_ub = [Buf("ub%d" % c) for c in range(NCH)]
        b_ust = Buf("ust")
        b_yT = [Buf("yT%d" % c) for c in range(NCH)]
        b_zT = [Buf("zT%d" % c) for c in range(NCH)]
        b_cst = [Buf("cst%d" % i) for i in range(4)]
        b_ysq = [Buf("ysq%d" % i) for i in range(2)]
        b_sst = Buf("sst")
        conv_bufs = b_ub + [b_ust] + b_yT + b_zT + b_cst + b_ysq + [b_sst]
        xin = scv(0, 4096).rearrange("p (k d) -> p k d", k=4)
        yfin = scv(4096, 4096).rearrange("p (c n) -> p c n", c=NCH)
        yout = scv(8192, 4096).rearrange("p (k d) -> p k d", k=4)
        cstage = scv(12288, 1024)
        b_xin = Buf("xin")
        b_yfin = [Buf("yfin%d" % c) for c in range(NCH)]
        b_yout = Buf("yout")
        b_cstage = Buf("cstage")
        io_bufs = [b_xin] + b_yfin + [b_yout, b_cstage]
        cur_view = [io_bufs]

        def switch(view):
            if cur_view[0] is not view:
                fence(cur_view[0], view)
                cur_view[0] = view

        out_bufs = []
        d_outs = {"y": S.dsem(), "kv": S.dsem(), "cv": S.dsem()}
        d_misc = S.dsem()
        d_xin = S.dsem()

        def out_dma(kind, fn, reads):
            b = Buf("out%d" % len(out_bufs))
            out_bufs.append(b)
            S.op("pool", fn, reads=reads, writes=[b], dsem=d_outs[kind])

        S.op("pool", lambda h: h.memset(ones[:], 1.0), writes=[b_const])
        S.op("pool", lambda h: h.memset(ident[:], 1.0), writes=[b_const])
        S.op("pool", lambda h: h.affine_select(out=ident[:], in_=ident[:], pattern=[[-1, 128]], compare_op=ALU.is_equal,
                                                fill=0.0, base=0, channel_multiplier=1), reads=[b_const], writes=[b_const])
        S.op("pool", lambda h: h.tensor_copy(out=identb[:], in_=ident[:]), reads=[b_const], writes=[b_const])
        ngrp = (NVEC * 8 + 127) // 128
        for g in range(ngrp):
            r0 = g * 128
            nr = min(128, NVEC * 8 - r0)
            S.op("sp", lambda h: h.dma_start(out=SC[0:nr, 512:640], in_=vecs[r0:r0 + nr, :]), writes=io_bufs, dsem=d_misc)
            S.op("pe", lambda h: h.transpose(out=banks[0][:, 0:nr], in_=SC[0:nr, 512:640], identity=ident[0:nr, 0:nr]),
                 reads=io_bufs + [b_const], writes=[b_bank[0]])
            S.op("dve", lambda h: h.tensor_copy(out=PAR[:, r0:r0 + nr], in_=banks[0][:, 0:nr]), reads=[b_bank[0]], writes=[b_par])
        nr = n_attn * 20
        S.op("sp", lambda h: h.dma_start(out=SC[0:nr, 512:576], in_=qkb[:, :]), writes=io_bufs, dsem=d_misc)
        S.op("pe", lambda h: h.transpose(out=banks[0][0:64, 0:nr], in_=SC[0:nr, 512:576], identity=ident[0:nr, 0:nr]),
             reads=io_bufs + [b_const], writes=[b_bank[0]])
        S.op("dve", lambda h: h.tensor_copy(out=QKB[:, :], in_=banks[0][0:64, 0:nr]), reads=[b_bank[0]], writes=[b_par])
        for a in range(n_attn):
            S.op("sp", lambda h: [h.dma_start(out=KVB[:, a, :], in_=kvb[a].partition_broadcast(128)),
                                  h.dma_start(out=SNK[:, a, :, 0], in_=sinks[a].partition_broadcast(128))],
                 writes=[b_par], dsem=d_misc)

        S.op("sp", lambda h: h.dma_start(out=biasT[:, :, :].rearrange("p h n -> p (h n)"), in_=alibi[:, :]), writes=[b_bias], dsem=S.dsem())
        def pcol(vec, c):
            return PAR[:, vec * 8 + c: vec * 8 + c + 1]
        V_N1, V_NM, V_N2, V_NF = 0, depth, 2 * depth, 3 * depth
        V_CBIN = 3 * depth + 1
        V_CBDW = V_CBIN + 2 * max(n_conv, 1)
        V_LNG = V_CBDW + max(n_conv, 1)
        V_LNB = V_LNG + max(n_conv, 1)
        V_CBO = V_LNB + max(n_conv, 1)
        V_WDW = V_CBO + max(n_conv, 1)

        b_sw = {}

        pending = []

        def precast(key, dst, src, rows, cols, after=()):
            b = Buf("sw_" + str(key))
            ds = S.dsem()
            b_sw[key] = b
            split = 1
            while cols // split > 2048:
                split *= 2
            step = 256
            for r in range(0, rows, step):
                r1 = min(rows, r + step)

                def chunk(r=r, r1=r1):
                    tok = S.op("pool", lambda h: h.dma_start(out=dst[r:r1, :].rearrange("r (s n) -> r s n", s=split),
                                                             in_=src[r:r1, :].rearrange("r (s n) -> r s n", s=split)),
                               writes=[Buf("tmp")], dsem=ds)
                    b.w = {id(tok[0]): tok}
                pending.append(chunk)

        def trickle(k=1):
            for _ in range(k):
                if pending:
                    pending.pop(0)()

        def emit_precast(l, after=()):
            precast(("in", 0, l), s_w_in[0][l], ffn_w_in[0][l], D, 2 * FF, after)
            precast(("out", 0, l), s_w_out[0][l], ffn_w_out[0][l], FF, D, after)
            if l % 2 == 0:
                a = l // 2
                precast(("qkv", a), s_qkv[a], attn_w_qkv[a], D, 1536, after)
                precast(("wo", a), s_wo[a], attn_w_o[a], D, D, after)
            else:
                c = l // 2
                precast(("cin", c), s_cin[c], conv_w_in[c], D, 2 * D, after)
                precast(("cout", c), s_cout[c], conv_w_out[c], D, D, after)
            precast(("in", 1, l), s_w_in[1][l], ffn_w_in[1][l], D, 2 * FF, after)
            precast(("out", 1, l), s_w_out[1][l], ffn_w_out[1][l], FF, D, after)

        diag_building = [False]
        d_diag = {}
        cur_tile = [0]

        def wsel(key, layer, fp32_ap, bf16_ap):
            if cur_tile[0] <= layer:
                return fp32_ap, "pool", []
            return bf16_ap, "sp", [b_sw[key]]

        def loadA(fn_dst_src, eng, deps):
            i = rA_i[0] % 3
            rA_i[0] += 1
            S.op(eng, lambda h: fn_dst_src(h, ringA[i]), reads=deps, writes=[b_rA[i]], dsem=d_rA[i])
            if rA_i[0] % 2 == 0:
                trickle()
            return i

        def loadB(fn_dst_src, eng, deps):
            i = rB_i[0] % 3
            rB_i[0] += 1
            S.op(eng, lambda h: fn_dst_src(h, ringB[i]), reads=deps, writes=[b_rB[i]], dsem=d_rB[i])
            if rB_i[0] % 2 == 0:
                trickle()
            return i

        def prenorm(n, gvec):
            done = set()

            def chunk(c, early=True):
                if c in done:
                    return
                done.add(c)
                switch(ffn_bufs)
                S.op("act", lambda h: h.activation(out=sq[:, c, :n], in_=xT[:, c, :n], func=AF.Square),
                     reads=[b_x[c]], writes=[b_sq[c]])
                S.op("dve", lambda h: h.tensor_scalar(out=hT[:, c, :n], in0=xT[:, c, :n], scalar1=pcol(gvec, c), scalar2=None, op0=ALU.mult),
                     reads=[b_x[c], b_par], writes=[b_h[c]])

            def finish():
                for c in range(NCH):
                    chunk(c, early=False)
                return tail()

            def tail():
                return _prenorm_tail(n)
            return chunk, finish

        def _prenorm_tail(n):
            S.op("dve", lambda h: h.tensor_tensor(out=ssum[:, :n], in0=sq[:, 0, :n], in1=sq[:, 1, :n], op=ALU.add),
                 reads=[b_sq[0], b_sq[1]], writes=[b_ssum])
            for c in range(2, NCH):
                S.op("dve", lambda h: h.tensor_tensor(out=ssum[:, :n], in0=ssum[:, :n], in1=sq[:, c, :n], op=ALU.add),
                     reads=[b_sq[c]], writes=[b_ssum])

            def stats():
                S.op("pe", lambda h: h.matmul(banks[7][:, :n], lhsT=ones[:], rhs=ssum[:, :n], start=True, stop=True),
                     reads=[b_ssum, b_const], writes=[b_bank[7]])
                S.op("act", lambda h: h.activation(out=rs[:, :n], in_=banks[7][:, :n], func=AF.Sqrt, scale=1.0 / D, bias=EPS),
                     reads=[b_bank[7]], writes=[b_rs])
                S.op("dve", lambda h: h.reciprocal(out=rs[:, :n], in_=rs[:, :n]), reads=[b_rs], writes=[b_rs])
            return stats

        def ffn(n, which, l, pre, nxt=None):
            stats = pre[1]()
            gsb = [sq[:, 4, :], sq[:, 5, :]]
            usb = [sq[:, 6, :], sq[:, 7, :]]
            b_gsb = [b_sq[4], b_sq[5]]
            b_usb = [b_sq[6], b_sq[7]]
            src_in, eng_in, dep_in = wsel(("in", which, l), l, ffn_w_in[which][l], s_w_in[which][l])
            src_out, eng_out, dep_out = wsel(("out", which, l), l, ffn_w_out[which][l], s_w_out[which][l])
            w_in = src_in.rearrange("(c p) (gu f) -> p c gu f", p=128, gu=2)
            w_out = src_out.rearrange("(j p) d -> p j d", p=128)

            def evacA(k, j):
                gb, ub = k % 3, 3 + k % 3
                S.op("dve", lambda h: h.tensor_tensor(out=gsb[k % 2][:, :n], in0=banks[gb][:, :n], in1=rs[:, :n], op=ALU.mult),
                     reads=[b_bank[gb], b_rs], writes=[b_gsb[k % 2]])
                S.op("act", lambda h: h.activation(out=sg[k % 2][:, :n], in_=gsb[k % 2][:, :n], func=AF.Silu),
                     reads=[b_gsb[k % 2]], writes=[b_sg[k % 2]])
                S.op("dve", lambda h: h.tensor_tensor(out=usb[k % 2][:, :n], in0=banks[ub][:, :n], in1=rs[:, :n], op=ALU.mult),
                     reads=[b_bank[ub], b_rs], writes=[b_usb[k % 2]])
                S.op("dve", lambda h: h.tensor_tensor(out=aT[:, j, :n], in0=usb[k % 2][:, :n], in1=sg[k % 2][:, :n], op=ALU.mult),
                     reads=[b_usb[k % 2], b_sg[k % 2]], writes=[b_aT[j]])

            k = 0
            for fg in range(6):
                j0 = fg * 4
                nj = min(4, NJ - j0)
                i = loadA(lambda h, dst: [h.dma_start(
                    out=dst[:, 0:NCH * 2 * nj * 128].rearrange("p (c gu f) -> p c gu f", c=NCH, gu=2)[:, :, gu, :],
                    in_=w_in[:, :, gu, j0 * 128:(j0 + nj) * 128]) for gu in range(2)], eng_in, dep_in)
                wv = ringA[i][:, 0:NCH * 2 * nj * 128].rearrange("p (c gu f) -> p c gu f", c=NCH, gu=2)
                for jj in range(nj):
                    j = j0 + jj
                    gb, ub = k % 3, 3 + k % 3
                    if k == 0:
                        for c in range(NCH):
                            S.op("pe", lambda h: h.matmul(banks[gb][:, :n], lhsT=wv[:, c, 0, jj * 128:(jj + 1) * 128], rhs=hT[:, c, :n],
                                                          start=(c == 0), stop=(c == NCH - 1)),
                                 reads=[b_rA[i], b_h[c]], writes=[b_bank[gb]])
                    else:
                        S.op("pe", lambda h: [h.matmul(banks[gb][:, :n], lhsT=wv[:, c, 0, jj * 128:(jj + 1) * 128], rhs=hT[:, c, :n],
                                                       start=(c == 0), stop=(c == NCH - 1)) for c in range(NCH)],
                             reads=[b_rA[i]] + b_h, writes=[b_bank[gb]])
                    S.op("pe", lambda h: [h.matmul(banks[ub][:, :n], lhsT=wv[:, c, 1, jj * 128:(jj + 1) * 128], rhs=hT[:, c, :n],
                                                   start=(c == 0), stop=(c == NCH - 1)) for c in range(NCH)],
                         reads=[b_rA[i]] + b_h, writes=[b_bank[ub]])
                    if k == 1:
                        stats()
                        evacA(0, 0)
                        evacA(1, 1)
                    elif k > 1:
                        evacA(k, j)
                    k += 1
            for half in range(2):
                bset = [4, 5, 6, 7] if half == 0 else [0, 1, 2, 3]
                for fgB in range(3):
                    j0 = fgB * 8
                    nj = min(8, NJ - j0)
                    i = loadB(lambda h, dst: h.dma_start(
                        out=dst[:, 0:nj * 512].rearrange("p (j d) -> p j d", j=nj),
                        in_=w_out[:, j0:j0 + nj, half * 512:(half + 1) * 512]), eng_out, dep_out)
                    wv = ringB[i][:, 0:nj * 512].rearrange("p (j d) -> p j d", j=nj)
                    S.op("pe", lambda h: [h.matmul(banks[bset[dd]][:, :n], lhsT=wv[:, jj, dd * 128:(dd + 1) * 128], rhs=aT[:, j0 + jj, :n],
                                                   start=(j0 + jj == 0), stop=(j0 + jj == NJ - 1))
                                          for jj in range(nj) for dd in range(4)],
                         reads=[b_rB[i]] + b_aT[j0:j0 + nj], writes=[b_bank[b] for b in bset])
                for dd in range(4):
                    c = half * 4 + dd
                    S.op("dve", lambda h: h.scalar_tensor_tensor(out=xT[:, c, :n], in0=banks[bset[dd]][:, :n], scalar=0.5,
                                                                 in1=xT[:, c, :n], op0=ALU.mult, op1=ALU.add),
                         reads=[b_bank[bset[dd]]], writes=[b_x[c]])
                    if nxt is not None:
                        nxt[0](c)

        ev = [0]

        def evac(out, in_, reads, writes, bias=None, eng=None):
            ev[0] += 1
            if eng == "act" or (eng is None and ev[0] % 2 == 0):
                if bias is None:
                    S.op("act", lambda h: h.copy(out=out, in_=in_), reads=reads, writes=writes)
                else:
                    S.op("act", lambda h: h.activation(out=out, in_=in_, func=AF.Identity, bias=bias, scale=1.0),
                         reads=reads + [b_par], writes=writes)
            else:
                if bias is None:
                    S.op("dve", lambda h: h.tensor_copy(out=out, in_=in_), reads=reads, writes=writes)
                else:
                    S.op("dve", lambda h: h.tensor_scalar(out=out, in0=in_, scalar1=bias, scalar2=None, op0=ALU.add),
                         reads=reads + [b_par], writes=writes)

        def attn_groups(a, groups):
            G = len(groups)

            def stA1(g):
                gr = groups[g]; r = g % 2
                nq, nkeys, h0 = gr["nq"], gr["nkeys"], gr["h0"]
                scb = PS[:, r * 1024:(r + 1) * 1024].rearrange("p (h n) -> p h n", h=4)
                bks = [b_bank[2 * r], b_bank[2 * r + 1]]
                S.op("pe", lambda h: [h.matmul(scb[0:nq, hh, 0:nkeys], lhsT=gr["q_ap"](h0 + hh), rhs=gr["k_ap"], start=True, stop=True) for hh in range(4)],
                     reads=b_qT[h0:h0 + 4] + [b_kT], writes=bks)
                S.op("dve", lambda h: h.scalar_tensor_tensor(out=Sp[r][0:nq, :, 0:nkeys], in0=scb[0:nq, :, 0:nkeys], scalar=HD ** -0.5,
                                                             in1=biasT[0:nq, h0:h0 + 4, 0:nkeys], op0=ALU.mult, op1=ALU.add),
                     reads=bks + [b_bias], writes=[b_Sp[r]])
                if gr["mask_first"]:
                    S.op("dve", lambda h: h.memset(Sp[r][0:nq, :, 0:128], NEG), reads=[b_Sp[r]], writes=[b_Sp[r]])

            def stA2(g):
                gr = groups[g]; r = g % 2
                nq, nkeys, h0 = gr["nq"], gr["nkeys"], gr["h0"]
                S.op("act", lambda h: h.copy(out=Sp[r][0:nq, :, nkeys:nkeys + 1], in_=SNK[0:nq, a, h0:h0 + 4, :]),
                     reads=[b_Sp[r], b_par], writes=[b_Sp[r]])

            def stA3(g):
                gr = groups[g]; r = g % 2
                nq, nkeys = gr["nq"], gr["nkeys"]
                S.op("dve", lambda h: h.tensor_reduce(out=small[0:nq, r, 0:4], in_=Sp[r][0:nq, :, 0:nkeys + 1], axis=AX.X, op=ALU.max, negate=True),
                     reads=[b_Sp[r]], writes=[b_small[r]])

            def stB1(g):
                gr = groups[g]; r = g % 2
                nq, nkeys = gr["nq"], gr["nkeys"]
                for hh in range(4):
                    S.op("act", lambda h: h.activation(out=Sp[r][0:nq, hh, 0:nkeys + 1], in_=Sp[r][0:nq, hh, 0:nkeys + 1], func=AF.Exp,
                                                       bias=small[0:nq, r, hh:hh + 1], scale=1.0, accum_out=small[0:nq, r, 4 + hh:5 + hh]),
                         reads=[b_Sp[r], b_small[r]], writes=[b_Sp[r], b_small[r]])

            def stB2(g):
                gr = groups[g]; r = g % 2
                nq, nkeys = gr["nq"], gr["nkeys"]
                S.op("dve", lambda h: h.reciprocal(out=small[0:nq, r, 8:12], in_=small[0:nq, r, 4:8]), reads=[b_small[r]], writes=[b_small[r]])
                S.op("dve", lambda h: h.tensor_tensor(out=Pn[r][0:nq, :, 0:nkeys], in0=Sp[r][0:nq, :, 0:nkeys],
                                                      in1=small[0:nq, r, 8:12].unsqueeze(2).to_broadcast([nq, 4, nkeys]), op=ALU.mult),
                     reads=[b_Sp[r], b_small[r]], writes=[b_Pn[r]])

            def stC(g):
                gr = groups[g]; r = g % 2
                nq = gr["nq"]
                ptb = banks[4 + r].bitcast(BF16).rearrange("p (h n) -> p h n", h=4)
                k0s = []
                k0 = 0
                for (v_ap, nk) in gr["vblocks"]:
                    k0s.append(k0)
                    k0 += nk
                S.op("pe", lambda h: [h.transpose(out=ptb[0:nk, hh, bi * 128:bi * 128 + nq], in_=Pn[r][0:nq, hh, k0s[bi]:k0s[bi] + nk],
                                                  identity=identb[0:nq, 0:nq]) for hh in range(4) for bi, (v_ap, nk) in enumerate(gr["vblocks"])],
                     reads=[b_Pn[r], b_const], writes=[b_bank[4 + r]])
                for bi, (v_ap, nk) in enumerate(gr["vblocks"]):
                    evac(PTs[r][0:nk, :, bi * 128:bi * 128 + nq], ptb[0:nk, :, bi * 128:bi * 128 + nq], [b_bank[4 + r]], [b_PTs[r]], eng="act")

            def stD(g):
                gr = groups[g]; r = g % 2
                nq, h0 = gr["nq"], gr["h0"]
                ob = banks[6 + r][0:64, :].rearrange("p (h n) -> p h n", h=4)
                nb = len(gr["vblocks"])
                S.op("pe", lambda h: [h.matmul(ob[:, hh, 0:nq], lhsT=v_ap, rhs=PTs[r][0:nk, hh, bi * 128:bi * 128 + nq],
                                               start=(bi == 0), stop=(bi == nb - 1)) for hh in range(4) for bi, (v_ap, nk) in enumerate(gr["vblocks"])],
                     reads=[b_PTs[r], b_V], writes=[b_bank[6 + r]])
                evac(gr["o_ap"], ob[:, :, 0:nq], [b_bank[6 + r]], b_oT[h0:h0 + 4], eng="act")

            for step in range(G + 3):
                if step < G:
                    stA1(step)
                if 0 <= step - 1 < G:
                    stB1(step - 1)
                if step < G:
                    stA2(step)
                if 0 <= step - 1 < G:
                    stB2(step - 1)
                if step < G:
                    stA3(step)
                if 0 <= step - 2 < G:
                    stC(step - 2)
                if 0 <= step - 3 < G:
                    stD(step - 3)

        def attn(n, l, prompt, b_idx, t_idx, stats):
            a = l // 2
            switch(attn_bufs)
            rmsnorm_attn = None
            src_q, eng_q, dep_q = wsel(("qkv", a), l, attn_w_qkv[a], s_qkv[a])
            wq = src_q.rearrange("(c p) n -> p c n", p=128)
            iq = loadA(lambda h, dst: h.dma_start(out=dst[:, :].rearrange("p (c n) -> p c n", c=NCH), in_=wq[:, :, 0:1024]), eng_q, dep_q)
            ikv = loadA(lambda h, dst: h.dma_start(out=dst[:, 0:4096].rearrange("p (c n) -> p c n", c=NCH), in_=wq[:, :, 1024:1536]), eng_q, dep_q)
            wqv = ringA[iq][:, :].rearrange("p (c n) -> p c n", c=NCH)
            wkv = ringA[ikv][:, 0:4096].rearrange("p (c n) -> p c n", c=NCH)
            if prompt:
                if t_idx == 0:
                    S.op("dve", lambda h: h.memset(kTt[:, :, 0:128], 0.0), writes=[b_kT])
                    S.op("dve", lambda h: h.memset(Vt[:, 0, :], 0.0), writes=[b_V])
                else:
                    S.op("dve", lambda h: h.tensor_copy(out=kTt[:, :, 0:128], in_=kcar[a][:, :, :]), reads=[b_kcar[a]], writes=[b_kT])
                    S.op("dve", lambda h: h.tensor_copy(out=Vt[:, 0, :], in_=vcar[a][:, :]), reads=[b_vcar[a]], writes=[b_V])
            def evacQ(hh):
                bk = hh % 4
                S.op("dve", lambda h: h.tensor_tensor(out=kvtok[hh % 2][0:64, :n], in0=banks[bk][0:64, :n], in1=rs[0:64, :n], op=ALU.mult),
                     reads=[b_bank[bk], b_rs], writes=[b_kvtok[hh % 2]])
                S.op("act", lambda h: h.activation(out=qT[:, hh, :n], in_=kvtok[hh % 2][0:64, :n], func=AF.Identity,
                                                   bias=QKB[:, a * 20 + hh:a * 20 + hh + 1], scale=1.0),
                     reads=[b_kvtok[hh % 2], b_par], writes=[b_qT[hh]])

            for hh in range(NH):
                bk = hh % 4
                if hh == 0:
                    for c in range(NCH):
                        S.op("pe", lambda h: h.matmul(banks[bk][0:64, :n], lhsT=wqv[:, c, hh * 64:(hh + 1) * 64], rhs=hT[:, c, :n],
                                                      start=(c == 0), stop=(c == NCH - 1)),
                             reads=[b_rA[iq], b_h[c]], writes=[b_bank[bk]])
                else:
                    S.op("pe", lambda h: [h.matmul(banks[bk][0:64, :n], lhsT=wqv[:, c, hh * 64:(hh + 1) * 64], rhs=hT[:, c, :n],
                                                   start=(c == 0), stop=(c == NCH - 1)) for c in range(NCH)],
                         reads=[b_rA[iq]] + b_h, writes=[b_bank[bk]])
                if hh == 1:
                    stats()
                    evacQ(0)
                    evacQ(1)
                elif hh > 1:
                    evacQ(hh)
            for kv in range(NKV):
                bk = kv % 4
                S.op("pe", lambda h: [h.matmul(banks[bk][0:64, :n], lhsT=wkv[:, c, kv * 64:(kv + 1) * 64], rhs=hT[:, c, :n],
                                               start=(c == 0), stop=(c == NCH - 1)) for c in range(NCH)],
                     reads=[b_rA[ikv]] + b_h, writes=[b_bank[bk]])
                S.op("dve", lambda h: h.tensor_tensor(out=kvtok[kv % 2][0:64, :n], in0=banks[bk][0:64, :n], in1=rs[0:64, :n], op=ALU.mult),
                     reads=[b_bank[bk], b_rs], writes=[b_kvtok[kv % 2]])
                S.op("act", lambda h: h.activation(out=kTt[:, kv, 128:128 + n], in_=kvtok[kv % 2][0:64, :n], func=AF.Identity,
                                                   bias=QKB[:, a * 20 + 16 + kv:a * 20 + 17 + kv], scale=1.0),
                     reads=[b_kvtok[kv % 2], b_par], writes=[b_kT])
            if prompt:
                blocks = [(blk * 128, 128) for blk in range(n // 128)]
            else:
                blocks = [(sb_ * dseq, dseq) for sb_ in range(2)]
            for bi, (t0, nt) in enumerate(blocks):
                bk = 4 + bi % 2
                kk = bi % 2
                S.op("pe", lambda h: [h.matmul(banks[bk][0:nt, :], lhsT=hT[:, c, t0:t0 + nt], rhs=wkv[:, c, :],
                                               start=(c == 0), stop=(c == NCH - 1)) for c in range(NCH)],
                     reads=[b_rA[ikv]] + b_h, writes=[b_bank[bk]])
                S.op("dve", lambda h: h.scalar_tensor_tensor(out=junk[0:nt, 0:nt], in0=rs[0:nt, t0:t0 + nt], scalar=1.0, in1=ident[0:nt, 0:nt],
                                                             op0=ALU.mult, op1=ALU.mult, accum_out=rtok[0:nt, bi:bi + 1]),
                     reads=[b_rs, b_const], writes=[b_junk, b_rtok])
                S.op("dve", lambda h: h.scalar_tensor_tensor(out=kvtok[kk][0:nt, :], in0=banks[bk][0:nt, :], scalar=rtok[0:nt, bi:bi + 1],
                                                             in1=KVB[0:nt, a, :], op0=ALU.mult, op1=ALU.add),
                     reads=[b_bank[bk], b_par, b_rtok], writes=[b_kvtok[kk]])
                if prompt:
                    S.op("act", lambda h: h.copy(out=Vt[:, 1 + bi, :], in_=kvtok[kk][:, 256:512]), reads=[b_kvtok[kk]], writes=[b_V])
                    if t_idx == NT - 1 and bi == len(blocks) - 1:
                        out_dma("kv", lambda h: [h.dma_start(out=nk_p[a, b_idx, :, :], in_=kvtok[kk][:, 0:256]),
                                                 h.dma_start(out=nv_p[a, b_idx, :, :], in_=kvtok[kk][:, 256:512])],
                                [b_kvtok[kk]])
                else:
                    S.op("act", lambda h: h.copy(out=Vt[0:nt, 1 + bi, :], in_=kvtok[kk][0:nt, 256:512]), reads=[b_kvtok[kk]], writes=[b_V])
                    out_dma("kv", lambda h: [h.dma_start(out=nk_s[a, bi, WIN - dseq:WIN, :], in_=kvtok[kk][0:nt, 0:256]),
                                             h.dma_start(out=nv_s[a, bi, WIN - dseq:WIN, :], in_=kvtok[kk][0:nt, 256:512]),
                                             h.dma_start(out=nk_s[a, bi, 0:WIN - dseq, :], in_=cache_k[a, bi, dseq:WIN, :]),
                                             h.dma_start(out=nv_s[a, bi, 0:WIN - dseq, :], in_=cache_v[a, bi, dseq:WIN, :])],
                            [b_kvtok[kk]])
            if prompt:
                if t_idx < NT - 1:
                    S.op("act", lambda h: h.copy(out=kcar[a][:, :, :], in_=kTt[:, :, 512:640]), reads=[b_kT], writes=[b_kcar[a]])
                    S.op("act", lambda h: h.copy(out=vcar[a][:, :], in_=Vt[:, 4, :]), reads=[b_V], writes=[b_vcar[a]])
                groups = []
                for blk in range(n // 128):
                    for kv in range(NKV):
                        groups.append(dict(nq=128, h0=kv * 4, q_ap=(lambda hh, blk=blk: qT[:, hh, blk * 128:(blk + 1) * 128]),
                                           k_ap=kTt[:, kv, blk * 128:blk * 128 + 256], nkeys=256,
                                           vblocks=[(Vt[:, blk, kv * 64:(kv + 1) * 64], 128), (Vt[:, blk + 1, kv * 64:(kv + 1) * 64], 128)],
                                           o_ap=oT[:, kv * 4:kv * 4 + 4, blk * 128:(blk + 1) * 128], mask_first=(t_idx == 0 and blk == 0)))
                attn_groups(a, groups)
            else:
                groups = []
                for sb_ in range(2):
                    kc0 = 256 + 192 * sb_
                    vblk = 0 if sb_ == 0 else 3
                    S.op("pool", lambda h: [h.dma_start(out=kstage[:, :], in_=cache_k[a, sb_, :, :]),
                                            h.dma_start(out=vstage[:, :], in_=cache_v[a, sb_, :, :])],
                         writes=[b_kstage, b_vstage], dsem=d_misc)
                    for kv in range(NKV):
                        S.op("pe", lambda h: h.transpose(out=banks[6][0:64, kv * 128:(kv + 1) * 128], in_=kstage[:, kv * 64:(kv + 1) * 64], identity=ident[:]),
                             reads=[b_kstage, b_const], writes=[b_bank[6]])
                    S.op("dve", lambda h: h.tensor_copy(out=kTt[:, :, kc0:kc0 + 128], in_=banks[6][0:64, :].rearrange("p (k n) -> p k n", k=NKV)),
                         reads=[b_bank[6]], writes=[b_kT])
                    S.op("dve", lambda h: h.tensor_copy(out=kTt[:, :, kc0 + 128:kc0 + 128 + dseq], in_=kTt[:, :, 128 + sb_ * dseq:128 + (sb_ + 1) * dseq]),
                         reads=[b_kT], writes=[b_kT])
                    S.op("act", lambda h: h.copy(out=Vt[:, vblk, :], in_=vstage[:, :]), reads=[b_vstage], writes=[b_V])
                    for kv in range(NKV):
                        groups.append(dict(nq=dseq, h0=kv * 4, q_ap=(lambda hh, sb_=sb_: qT[:, hh, sb_ * dseq:(sb_ + 1) * dseq]),
                                           k_ap=kTt[:, kv, kc0:kc0 + 128 + dseq], nkeys=WIN + dseq,
                                           vblocks=[(Vt[:, vblk, kv * 64:(kv + 1) * 64], 128), (Vt[0:dseq, 1 + sb_, kv * 64:(kv + 1) * 64], dseq)],
                                           o_ap=oT[:, kv * 4:kv * 4 + 4, sb_ * dseq:(sb_ + 1) * dseq], mask_first=False))
                attn_groups(a, groups)
            src_o, eng_o, dep_o = wsel(("wo", a), l, attn_w_o[a], s_wo[a])
            wo = src_o.rearrange("(hh p) n -> p hh n", p=64)
            for half in range(2):
                i = loadA(lambda h, dst: h.dma_start(out=dst[0:64, :].rearrange("p (hh n) -> p hh n", hh=NH),
                                                     in_=wo[:, :, half * 512:(half + 1) * 512]), eng_o, dep_o)
                wv = ringA[i][0:64, :].rearrange("p (hh n) -> p hh n", hh=NH)
                for dd in range(4):
                    c = half * 4 + dd
                    bk = 6 + dd % 2
                    S.op("pe", lambda h: [h.matmul(banks[bk][:, :n], lhsT=wv[:, hh, dd * 128:(dd + 1) * 128], rhs=oT[:, hh, :n],
                                                   start=(hh == 0), stop=(hh == NH - 1)) for hh in range(NH)],
                         reads=[b_rA[i]] + b_oT, writes=[b_bank[bk]])
                    S.op("dve", lambda h: h.tensor_tensor(out=xT[:, c, :n], in0=banks[bk][:, :n], in1=xT[:, c, :n], op=ALU.add),
                         reads=[b_bank[bk]], writes=[b_x[c]])

        def conv(n, l, prompt, b_idx, t_idx, stats):
            cl = l // 2
            switch(conv_bufs)
            nseq = 1 if prompt else 2
            L = n // nseq
            UL = CP + L
            ubv = ub[:, :, 0:nseq * UL].rearrange("p c (s u) -> p c s u", s=nseq)
            need_state = (not prompt) or t_idx == NT - 1

            def tokv(ap2):
                return ap2.rearrange("p (s t) -> p s t", s=nseq)
            src_ci, eng_ci, dep_ci = wsel(("cin", cl), l, conv_w_in[cl], s_cin[cl])
            win = src_ci.rearrange("(c p) (ag f) -> p c ag f", p=128, ag=2)
            if prompt:
                if t_idx == 0:
                    S.op("dve", lambda h: h.memset(ubv[:, :, :, 0:CP], 0.0), writes=b_ub)
                else:
                    S.op("dve", lambda h: h.tensor_copy(out=ubv[:, :, 0, 0:CP], in_=ucar[cl][:, :, :]), reads=[b_ucar[cl]], writes=b_ub)
            else:
                for sb_ in range(2):
                    S.op("pool", lambda h: h.dma_start(out=sst[0:CP, :], in_=state_conv[cl, sb_, :, :]), writes=[b_sst], dsem=d_misc)
                    for half in range(2):
                        S.op("pe", lambda h: [h.transpose(out=banks[6 + half][:, jj * 32:jj * 32 + CP], in_=sst[0:CP, (half * 4 + jj) * 128:(half * 4 + jj + 1) * 128],
                                                          identity=ident[0:CP, 0:CP]) for jj in range(4)],
                             reads=[b_sst, b_const], writes=[b_bank[6 + half]])
                        S.op("dve", lambda h: h.tensor_copy(out=ubv[:, half * 4:half * 4 + 4, sb_, 0:CP],
                                                            in_=banks[6 + half][:, 0:128].rearrange("p (j t) -> p j t", j=4)[:, :, 0:CP]),
                             reads=[b_bank[6 + half]], writes=b_ub[half * 4:half * 4 + 4])
            def evacG(k, j):
                ab, gb = k % 3, 3 + k % 3
                sgm = cst[2 + k % 2]
                av = ysq[k % 2]
                S.op("dve", lambda h: h.tensor_tensor(out=sgm[:, :n], in0=banks[gb][:, :n], in1=rs[:, :n], op=ALU.mult),
                     reads=[b_bank[gb], b_rs], writes=[b_cst[2 + k % 2]])
                S.op("act", lambda h: h.activation(out=sgm[:, :n], in_=sgm[:, :n], func=AF.Sigmoid,
                                                   bias=pcol(V_CBIN + 2 * cl + 1, j), scale=1.0),
                     reads=[b_par], writes=[b_cst[2 + k % 2]])
                S.op("dve", lambda h: h.tensor_tensor(out=av[:, :n], in0=banks[ab][:, :n], in1=rs[:, :n], op=ALU.mult),
                     reads=[b_bank[ab], b_rs], writes=[b_ysq[k % 2]])
                S.op("dve", lambda h: h.scalar_tensor_tensor(out=ubv[:, j, :, CP:UL], in0=tokv(av[:, :n]),
                                                             scalar=pcol(V_CBIN + 2 * cl, j), in1=tokv(sgm[:, :n]),
                                                             op0=ALU.add, op1=ALU.mult),
                     reads=[b_ysq[k % 2], b_cst[2 + k % 2], b_par], writes=[b_ub[j]])
                if need_state:
                    S.op("dve", lambda h: h.scalar_tensor_tensor(out=ust[:, j, 0:nseq, :], in0=tokv(av[:, :n])[:, :, L - CP:L],
                                                                 scalar=pcol(V_CBIN + 2 * cl, j), in1=tokv(sgm[:, :n])[:, :, L - CP:L],
                                                                 op0=ALU.add, op1=ALU.mult),
                         reads=[b_ysq[k % 2], b_cst[2 + k % 2], b_par], writes=[b_ust])

            k = 0
            for piece in range(2):
                i = loadA(lambda h, dst: [h.dma_start(out=dst[:, :].rearrange("p (c ag f) -> p c ag f", c=NCH, ag=2)[:, :, ag, :],
                                                      in_=win[:, :, ag, piece * 512:(piece + 1) * 512]) for ag in range(2)], eng_ci, dep_ci)
                wv = ringA[i][:, :].rearrange("p (c ag f) -> p c ag f", c=NCH, ag=2)
                for jj in range(4):
                    j = piece * 4 + jj
                    ab, gb = k % 3, 3 + k % 3
                    if k == 0:
                        for c in range(NCH):
                            S.op("pe", lambda h: h.matmul(banks[ab][:, :n], lhsT=wv[:, c, 0, jj * 128:(jj + 1) * 128], rhs=hT[:, c, :n],
                                                          start=(c == 0), stop=(c == NCH - 1)),
                                 reads=[b_rA[i], b_h[c]], writes=[b_bank[ab]])
                    else:
                        S.op("pe", lambda h: [h.matmul(banks[ab][:, :n], lhsT=wv[:, c, 0, jj * 128:(jj + 1) * 128], rhs=hT[:, c, :n],
                                                       start=(c == 0), stop=(c == NCH - 1)) for c in range(NCH)],
                             reads=[b_rA[i]] + b_h, writes=[b_bank[ab]])
                    S.op("pe", lambda h: [h.matmul(banks[gb][:, :n], lhsT=wv[:, c, 1, jj * 128:(jj + 1) * 128], rhs=hT[:, c, :n],
                                                   start=(c == 0), stop=(c == NCH - 1)) for c in range(NCH)],
                         reads=[b_rA[i]] + b_h, writes=[b_bank[gb]])
                    if k == 1:
                        stats()
                        evacG(0, 0)
                        evacG(1, 1)
                    elif k > 1:
                        evacG(k, j)
                    k += 1
            if prompt and t_idx < NT - 1:
                S.op("act", lambda h: h.copy(out=ucar[cl][:, :, :], in_=ubv[:, :, 0, L:UL]), reads=b_ub, writes=[b_ucar[cl]])
            if need_state:
                for sb_ in range(nseq):
                    for half in range(2):
                        S.op("pe", lambda h: [h.transpose(out=banks[6 + half][0:CP, jj * 128:(jj + 1) * 128], in_=ust[:, half * 4 + jj, sb_, :],
                                                          identity=ident[:]) for jj in range(4)],
                             reads=[b_ust, b_const], writes=[b_bank[6 + half]])
                        S.op("act", lambda h: h.copy(out=sst[0:CP, half * 512:(half + 1) * 512], in_=banks[6 + half][0:CP, :]),
                             reads=[b_bank[6 + half]], writes=[b_sst])
                    dst = ncv_p[cl, b_idx, :, :] if prompt else ncv_s[cl, sb_, :, :]
                    out_dma("cv", lambda h: h.dma_start(out=dst, in_=sst[0:CP, :]), [b_sst])
            for pr in range(4):
                if ("diag", cl) in b_sw and not diag_building[0]:
                    i = loadA(lambda h, dst: h.dma_start(out=dst[:, 0:2 * CW * 128], in_=s_diag[cl][:, pr * 2 * CW * 128:(pr + 1) * 2 * CW * 128]),
                              "sp", [b_sw[("diag", cl)]])
                else:
                    if pr == 0:
                        diag_building[0] = True
                        b_sw[("diag", cl)] = Buf("sw_diag%d" % cl)
                        d_diag[cl] = S.dsem()
                    i = rA_i[0] % 3
                    rA_i[0] += 1
                    S.op("dve", lambda h: [h.tensor_scalar(out=ringA[i][:, (jj * CW + kk) * 128:(jj * CW + kk + 1) * 128], in0=identb[:],
                                                           scalar1=pcol(V_WDW + cl * CW + kk, pr * 2 + jj), scalar2=None, op0=ALU.mult)
                                           for jj in range(2) for kk in range(CW)],
                         reads=[b_const, b_par], writes=[b_rA[i]])
                    S.op("sp", lambda h: h.dma_start(out=s_diag[cl][:, pr * 2 * CW * 128:(pr + 1) * 2 * CW * 128], in_=ringA[i][:, 0:2 * CW * 128]),
                         reads=[b_rA[i]], writes=[b_sw[("diag", cl)]], dsem=d_diag[cl])
                    if pr == 3:
                        diag_building[0] = False
                for jj in range(2):
                    j = pr * 2 + jj
                    cb = j % 2
                    S.op("pe", lambda h: [h.matmul(tokv(banks[cb][:, :n]), lhsT=ringA[i][:, (jj * CW + kk) * 128:(jj * CW + kk + 1) * 128],
                                                   rhs=ubv[:, j, :, kk:kk + L], start=(kk == 0), stop=(kk == CW - 1)) for kk in range(CW)],
                         reads=[b_rA[i], b_ub[j]], writes=[b_bank[cb]])
                    S.op("act", lambda h: h.activation(out=yT[:, j, :n], in_=banks[cb][:, :n], func=AF.Identity, bias=pcol(V_CBDW + cl, j), scale=1.0),
                         reads=[b_bank[cb], b_par], writes=[b_yT[j]])
                    S.op("act", lambda h: h.activation(out=ysq[j % 2][:, :n], in_=banks[cb][:, :n], func=AF.Square, bias=pcol(V_CBDW + cl, j), scale=1.0),
                         reads=[b_bank[cb], b_par], writes=[b_ysq[j % 2]])
                    if j == 0:
                        S.op("dve", lambda h: h.tensor_copy(out=cst[2][:, :n], in_=yT[:, j, :n]), reads=[b_yT[j]], writes=[b_cst[2]])
                        S.op("dve", lambda h: h.tensor_copy(out=cst[3][:, :n], in_=ysq[j % 2][:, :n]), reads=[b_ysq[j % 2]], writes=[b_cst[3]])
                    else:
                        S.op("dve", lambda h: h.tensor_tensor(out=cst[2][:, :n], in0=cst[2][:, :n], in1=yT[:, j, :n], op=ALU.add),
                             reads=[b_yT[j]], writes=[b_cst[2]])
                        S.op("dve", lambda h: h.tensor_tensor(out=cst[3][:, :n], in0=cst[3][:, :n], in1=ysq[j % 2][:, :n], op=ALU.add),
                             reads=[b_ysq[j % 2]], writes=[b_cst[3]])
            S.op("pe", lambda h: h.matmul(banks[4][:, :n], lhsT=ones[:], rhs=cst[2][:, :n], start=True, stop=True),
                 reads=[b_cst[2], b_const], writes=[b_bank[4]])
            S.op("pe", lambda h: h.matmul(banks[5][:, :n], lhsT=ones[:], rhs=cst[3][:, :n], start=True, stop=True),
                 reads=[b_cst[3], b_const], writes=[b_bank[5]])
            mean, var = cst[0], cst[1]
            S.op("act", lambda h: h.activation(out=mean[:, :n], in_=banks[4][:, :n], func=AF.Identity, scale=1.0 / D, bias=0.0),
                 reads=[b_bank[4]], writes=[b_cst[0]])
            S.op("dve", lambda h: h.tensor_tensor(out=var[:, :n], in0=mean[:, :n], in1=mean[:, :n], op=ALU.mult), reads=[b_cst[0]], writes=[b_cst[1]])
            S.op("dve", lambda h: h.scalar_tensor_tensor(out=var[:, :n], in0=banks[5][:, :n], scalar=1.0 / D, in1=var[:, :n],
                                                         op0=ALU.mult, op1=ALU.subtract), reads=[b_bank[5], b_cst[1]], writes=[b_cst[1]])
            S.op("act", lambda h: h.activation(out=var[:, :n], in_=var[:, :n], func=AF.Sqrt, scale=1.0, bias=EPS), reads=[b_cst[1]], writes=[b_cst[1]])
            S.op("dve", lambda h: h.reciprocal(out=var[:, :n], in_=var[:, :n]), reads=[b_cst[1]], writes=[b_cst[1]])
            for j in range(NCH):
                S.op("dve", lambda h: h.tensor_tensor(out=yT[:, j, :n], in0=yT[:, j, :n], in1=mean[:, :n], op=ALU.subtract),
                     reads=[b_cst[0]], writes=[b_yT[j]])
                S.op("dve", lambda h: h.tensor_tensor(out=yT[:, j, :n], in0=yT[:, j, :n], in1=var[:, :n], op=ALU.mult),
                     reads=[b_cst[1]], writes=[b_yT[j]])
                S.op("act", lambda h: h.activation(out=zT[:, j, :n], in_=yT[:, j, :n], func=AF.Silu, scale=pcol(V_LNG + cl, j), bias=pcol(V_LNB + cl, j)),
                     reads=[b_yT[j], b_par], writes=[b_zT[j]])
            src_co, eng_co, dep_co = wsel(("cout", cl), l, conv_w_out[cl], s_cout[cl])
            wout = src_co.rearrange("(c p) n -> p c n", p=128)
            i = loadA(lambda h, dst: h.dma_start(out=dst[:, :].rearrange("p (c n) -> p c n", c=NCH), in_=wout[:, :, :]), eng_co, dep_co)
            wv = ringA[i][:, :].rearrange("p (c n) -> p c n", c=NCH)
            for dd in range(NCH):
                bk = 6 + dd % 2
                S.op("pe", lambda h: [h.matmul(banks[bk][:, :n], lhsT=wv[:, c, dd * 128:(dd + 1) * 128], rhs=zT[:, c, :n],
                                               start=(c == 0), stop=(c == NCH - 1)) for c in range(NCH)],
                     reads=[b_rA[i]] + b_zT, writes=[b_bank[bk]])
                S.op("dve", lambda h: h.scalar_tensor_tensor(out=xT[:, dd, :n], in0=banks[bk][:, :n], scalar=pcol(V_CBO + cl, dd),
                                                             in1=xT[:, dd, :n], op0=ALU.add, op1=ALU.add),
                     reads=[b_bank[bk], b_par], writes=[b_x[dd]])

        def load_x(n, src_rows):
            switch(io_bufs)
            S.op("pool", lambda h: [h.dma_start(out=xin[0:nt, bi, :], in_=ap) for bi, (ap, nt) in enumerate(src_rows)],
                 writes=[b_xin], dsem=d_xin)
            t0 = 0
            for bi, (ap, nt) in enumerate(src_rows):
                for half in range(2):
                    bk = (bi * 2 + half) % 4
                    S.op("pe", lambda h: [h.transpose(out=banks[bk][:, jj * 128:jj * 128 + nt], in_=xin[0:nt, bi, (half * 4 + jj) * 128:(half * 4 + jj + 1) * 128],
                                                      identity=ident[0:nt, 0:nt]) for jj in range(4)],
                         reads=[b_xin, b_const], writes=[b_bank[bk]])
                    evac(xT[:, half * 4:half * 4 + 4, t0:t0 + nt], banks[bk][:, :].rearrange("p (j t) -> p j t", j=4)[:, :, 0:nt],
                         [b_bank[bk]], b_x[half * 4:half * 4 + 4])
                t0 += nt

        def store_y(n, dst_rows):
            switch(io_bufs)
            for c in range(NCH):
                S.op("act", lambda h: h.activation(out=sq[:, c, :n], in_=xT[:, c, :n], func=AF.Square), reads=[b_x[c]], writes=[b_xin])
            S.op("dve", lambda h: h.tensor_tensor(out=ssum[:, :n], in0=sq[:, 0, :n], in1=sq[:, 1, :n], op=ALU.add), reads=[b_xin], writes=[b_ssum])
            for c in range(2, NCH):
                S.op("dve", lambda h: h.tensor_tensor(out=ssum[:, :n], in0=ssum[:, :n], in1=sq[:, c, :n], op=ALU.add), reads=[b_xin], writes=[b_ssum])
            S.op("pe", lambda h: h.matmul(banks[7][:, :n], lhsT=ones[:], rhs=ssum[:, :n], start=True, stop=True),
                 reads=[b_ssum, b_const], writes=[b_bank[7]])
            S.op("act", lambda h: h.activation(out=rs[:, :n], in_=banks[7][:, :n], func=AF.Sqrt, scale=1.0 / D, bias=EPS),
                 reads=[b_bank[7]], writes=[b_rs])
            S.op("dve", lambda h: h.reciprocal(out=rs[:, :n], in_=rs[:, :n]), reads=[b_rs], writes=[b_rs])
            for c in range(NCH):
                S.op("dve", lambda h: h.scalar_tensor_tensor(out=yfin[:, c, :n], in0=xT[:, c, :n], scalar=pcol(V_NF, c),
                                                             in1=rs[:, :n], op0=ALU.mult, op1=ALU.mult),
                     reads=[b_x[c], b_rs, b_par], writes=[b_yfin[c]])
            t0 = 0
            for bi, (ap, nt) in enumerate(dst_rows):
                for half in range(2):
                    bk = (bi * 2 + half) % 4
                    S.op("pe", lambda h: [h.transpose(out=banks[bk][0:nt, jj * 128:(jj + 1) * 128], in_=yfin[:, half * 4 + jj, t0:t0 + nt],
                                                      identity=ident[:]) for jj in range(4)],
                         reads=b_yfin + [b_const], writes=[b_bank[bk]])
                    evac(yout[0:nt, bi, half * 512:(half + 1) * 512], banks[bk][0:nt, :], [b_bank[bk]], [b_yout])
                t0 += nt
            out_dma("y", lambda h: [h.dma_start(out=ap, in_=yout[0:nt, bi, :]) for bi, (ap, nt) in enumerate(dst_rows)], [b_yout])

        first_tile = [True]

        def run_tile(n, prompt, b_idx, t_idx, rows_in, rows_out):
            load_x(n, rows_in)
            nxt = None
            for l in range(depth):
                pre1 = nxt if nxt is not None else prenorm(n, V_N1 + l)
                prem = prenorm(n, V_NM + l)
                ffn(n, 0, l, pre1, nxt=prem)
                if l == 0 and prompt and cur_tile[0] < depth:
                    emit_precast(cur_tile[0])
                stats = prem[1]()
                if l % 2 == 0:
                    attn(n, l, prompt, b_idx, t_idx, stats)
                else:
                    conv(n, l, prompt, b_idx, t_idx, stats)
                nxt = prenorm(n, V_N1 + l + 1) if l + 1 < depth else None
                ffn(n, 1, l, prenorm(n, V_N2 + l), nxt=nxt)
            store_y(n, rows_out)
            trickle(len(pending))
            cur_tile[0] += 1

        for b_idx in range(2):
            for t_idx in range(NT):
                rin = [(x_prompt[b_idx, t_idx * 512 + k * 128:t_idx * 512 + (k + 1) * 128, :], 128) for k in range(4)]
                rout = [(y_prompt[b_idx, t_idx * 512 + k * 128:t_idx * 512 + (k + 1) * 128, :], 128) for k in range(4)]
                run_tile(512, True, b_idx, t_idx, rin, rout)
        cur_tile[0] = 10 ** 6
        if cfg.get("sample", True):
            rin = [(x_sample.rearrange("b t d -> (b t) d")[:, :], NS)]
            rout = [(y_sample.rearrange("b t d -> (b t) d")[:, :], NS)]
            run_tile(NS, False, 0, 0, rin, rout)

        S.wait_all("pool", out_bufs)
        build.stats = (S.nops, S.nwaits, S.n_dsem)
    return nc


def _pack(inputs, cfg):
    depth = cfg["depth"]
    n_attn = (depth + 1) // 2
    n_conv = depth // 2
    f = lambda k: np.ascontiguousarray(np.asarray(inputs[k], dtype=np.float32))
    nconv1 = max(n_conv, 1)
    cb_in = f("conv_b_in")[:nconv1].reshape(nconv1 * 2, D)
    rows = [f("norm_ffn1")[:depth], f("norm_mix")[:depth], f("norm_ffn2")[:depth], f("norm_final").reshape(1, D),
            cb_in, f("conv_b_dw")[:nconv1], f("conv_ln_g")[:nconv1], f("conv_ln_b")[:nconv1], f("conv_b_out")[:nconv1],
            f("conv_w_dw")[:nconv1].reshape(nconv1 * CW, D)]
    vecs = np.ascontiguousarray(np.concatenate(rows, axis=0).reshape(-1, 128))
    bq = f("attn_b_qkv")[:n_attn]
    qkb = np.ascontiguousarray(bq[:, :1280].reshape(n_attn * 20, 64))
    kvb = np.ascontiguousarray(bq[:, 1024:1536].reshape(n_attn, 1, 512))
    sinks = np.ascontiguousarray(f("attn_sinks")[:n_attn].reshape(n_attn, 1, NH))
    qi = np.arange(128, dtype=np.float32)[:, None]
    si = np.arange(256, dtype=np.float32)[None, :]
    dist = np.abs(np.float32(128.0) + qi - si).astype(np.float32)
    valid = ((qi < 64) & (si < 192)) | ((qi >= 64) & (si >= 64))
    slopes = np.array([2.0 ** (-8.0 * (h + 1) / NH) for h in range(NH)], dtype=np.float32)
    alibi = (-slopes[None, :, None] * dist[:, None, :]).astype(np.float32)
    alibi = np.where(valid[:, None, :], alibi, np.float32(NEG)).astype(np.float32)
    shared = {
        "alibi": np.ascontiguousarray(alibi.reshape(128, NH * 256)),
        "vecs": vecs, "qkb": qkb, "kvb": kvb, "sinks": sinks,
        "ffn1_w_in": f("ffn1_w_in")[:depth], "ffn2_w_in": f("ffn2_w_in")[:depth],
        "ffn1_w_out": f("ffn1_w_out")[:depth], "ffn2_w_out": f("ffn2_w_out")[:depth],
        "attn_w_qkv": f("attn_w_qkv")[:n_attn], "attn_w_o": f("attn_w_o")[:n_attn],
        "conv_w_in": f("conv_w_in")[:nconv1], "conv_w_out": f("conv_w_out")[:nconv1],
    }
    xp, xs = f("x_prompt"), f("x_sample")
    ck = f("cache_k")[:n_attn].reshape(n_attn, -1, WIN, 256)
    cv = f("cache_v")[:n_attn].reshape(n_attn, -1, WIN, 256)
    scv = f("state_conv")[:nconv1]
    maps = []
    for c in range(N_CORES):
        m = dict(shared)
        m["x_prompt"] = np.ascontiguousarray(xp[2 * c:2 * c + 2])
        m["x_sample"] = np.ascontiguousarray(xs[2 * c:2 * c + 2])
        m["cache_k"] = np.ascontiguousarray(ck[:, 2 * c:2 * c + 2])
        m["cache_v"] = np.ascontiguousarray(cv[:, 2 * c:2 * c + 2])
        m["state_conv"] = np.ascontiguousarray(scv[:, 2 * c:2 * c + 2])
        maps.append(m)
    return maps


def run(inputs, cfg):
    nc = build(cfg)
    maps = _pack(inputs, cfg)
    res = run_bass_kernel_spmd(nc, maps, core_ids=list(range(N_CORES)))
    R = res.results
    depth = cfg["depth"]
    n_attn = (depth + 1) // 2
    cat0 = lambda k: np.concatenate([r[k] for r in R], axis=0)
    cat1 = lambda k: np.concatenate([r[k] for r in R], axis=1)
    y_p = cat0("y_prompt")
    y_s = cat0("y_sample")
    kp = cat1("new_k_prompt").reshape(n_attn, -1, WIN, NKV, HD)
    vp = cat1("new_v_prompt").reshape(n_attn, -1, WIN, NKV, HD)
    cp = cat1("new_conv_prompt")
    ks = cat1("new_k_sample").reshape(n_attn, -1, WIN, NKV, HD)
    vs = cat1("new_v_sample").reshape(n_attn, -1, WIN, NKV, HD)
    cs = cat1("new_conv_sample")
    return tuple(np.ascontiguousarray(a, dtype=np.float32) for a in (y_p, y_s, kp, vp, cp, ks, vs, cs))


def kernel(**inputs):
    cfg = {"depth": 4, "seq": 2048, "dseq": 32}
    return run(inputs, cfg)
```

```python
from contextlib import ExitStack
import numpy as np
import concourse.bass as bass
import concourse.mybir as mybir
from concourse.bass_utils import run_bass_kernel_spmd

F32 = mybir.dt.float32
BF16 = mybir.dt.bfloat16
I32 = mybir.dt.int32
AF = mybir.ActivationFunctionType
ALU = mybir.AluOpType
AX = mybir.AxisListType

D = 1024
FF = 2816
NCH = 8
NJ = 22
NH = 16
NKV = 4
HD = 64
CW = 31
CP = 30
WIN = 128
EPS = 1e-5
NEG = -1e30
N_CORES = 8


class Buf:
    __slots__ = ("name", "w", "r")

    def __init__(self, name):
        self.name = name
        self.w = {}
        self.r = {}


class Eng:
    def __init__(self, name, h, sem):
        self.name = name
        self.h = h
        self.sem = sem
        self.count = 0
        self.waited = {}


class DSem:
    def __init__(self, sem):
        self.sem = sem
        self.val = 0


class Sched:
    def __init__(self, nc, stack):
        self.nc = nc
        self.stack = stack
        self.eng = {}
        for name, h in (("pe", nc.tensor), ("act", nc.scalar), ("dve", nc.vector),
                        ("pool", nc.gpsimd), ("sp", nc.sync)):
            sem = stack.enter_context(nc.semaphore("sem_" + name))
            self.eng[name] = Eng(name, h, sem)
        self.n_dsem = 0
        self.nops = 0
        self.nwaits = 0

    def dsem(self):
        self.n_dsem += 1
        return DSem(self.stack.enter_context(self.nc.semaphore("dsem%d" % self.n_dsem)))

    def _need(self, e, reads, writes):
        need = {}
        for b in reads:
            for k, (sem, val) in b.w.items():
                if e.waited.get(k, 0) < val and need.get(k, (None, 0))[1] < val:
                    need[k] = (sem, val)
        for b in writes:
            for d in (b.w, b.r):
                for k, (sem, val) in d.items():
                    if e.waited.get(k, 0) < val and need.get(k, (None, 0))[1] < val:
                        need[k] = (sem, val)
        return need

    def op(self, ename, fn, reads=(), writes=(), dsem=None, after=()):
        e = self.eng[ename]
        need = self._need(e, list(reads) + list(after), writes)
        if ename == "pe":
            need.pop(id(e.sem), None)
        items = list(need.items())
        attach = None
        if items and ename != "pe":
            attach = items.pop()
        for k, (sem, val) in items:
            e.h.wait_ge(sem, val)
            e.waited[k] = val
            self.nwaits += 1
        res = fn(e.h)
        inss = list(res) if isinstance(res, (list, tuple)) else [res]
        if attach is not None:
            k, (sem, val) = attach
            inss[0].wait_op(sem, val, "sem-ge")
            e.waited[k] = val
            self.nwaits += 1
        if dsem is not None:
            for i in inss:
                i.then_inc(dsem.sem, 16)
            dsem.val += 16 * len(inss)
            tok = (dsem.sem, dsem.val)
        else:
            e.count += 1
            inss[-1].then_inc(e.sem, 1)
            tok = (e.sem, e.count)
        key = id(tok[0])
        for b in writes:
            b.w = {key: tok}
            b.r = {}
        for b in reads:
            if b not in writes:
                b.r[key] = tok
        self.nops += 1
        return tok

    def wait_all(self, ename, bufs):
        e = self.eng[ename]
        need = self._need(e, (), bufs)
        for k, (sem, val) in need.items():
            e.h.wait_ge(sem, val)
            e.waited[k] = val


def fence(old_bufs, new_bufs):
    comb = {}
    for b in old_bufs:
        for d in (b.w, b.r):
            for k, (sem, val) in d.items():
                if comb.get(k, (None, 0))[1] < val:
                    comb[k] = (sem, val)
    for b in new_bufs:
        b.w = dict(comb)
        b.r = {}


def build(cfg):
    depth = cfg["depth"]
    seq = cfg["seq"]
    dseq = cfg["dseq"]
    NT = seq // 512
    n_attn = (depth + 1) // 2
    n_conv = depth // 2
    NS = 2 * dseq

    nc = bass.Bass("TRN2", target_bir_lowering=False)

    def din(name, shape):
        return nc.dram_tensor(name, list(shape), F32, kind="ExternalInput").ap()

    def dout(name, shape):
        return nc.dram_tensor(name, list(shape), F32, kind="ExternalOutput").ap()

    def dscr(name, shape):
        return nc.dram_tensor(name, list(shape), BF16, kind="Internal").ap()

    x_prompt = din("x_prompt", (2, seq, D))
    x_sample = din("x_sample", (2, dseq, D))
    cache_k = din("cache_k", (n_attn, 2, WIN, NKV * HD))
    cache_v = din("cache_v", (n_attn, 2, WIN, NKV * HD))
    state_conv = din("state_conv", (max(n_conv, 1), 2, CP, D))
    NVEC = 3 * depth + 1 + max(n_conv, 1) * (2 + 4 + CW)
    vecs = din("vecs", (NVEC * 8, 128))
    qkb = din("qkb", (n_attn * 20, 64))
    kvb = din("kvb", (n_attn, 1, 512))
    sinks = din("sinks", (n_attn, 1, NH))
    alibi = din("alibi", (128, NH * 256))
    ffn_w_in = [din("ffn1_w_in", (depth, D, 2 * FF)), din("ffn2_w_in", (depth, D, 2 * FF))]
    ffn_w_out = [din("ffn1_w_out", (depth, FF, D)), din("ffn2_w_out", (depth, FF, D))]
    attn_w_qkv = din("attn_w_qkv", (n_attn, D, 1536))
    attn_w_o = din("attn_w_o", (n_attn, D, D))
    conv_w_in = din("conv_w_in", (max(n_conv, 1), D, 2 * D))
    conv_w_out = din("conv_w_out", (max(n_conv, 1), D, D))

    y_prompt = dout("y_prompt", (2, seq, D))
    y_sample = dout("y_sample", (2, dseq, D))
    nk_p = dout("new_k_prompt", (n_attn, 2, WIN, 256))
    nv_p = dout("new_v_prompt", (n_attn, 2, WIN, 256))
    ncv_p = dout("new_conv_prompt", (max(n_conv, 1), 2, CP, D))
    nk_s = dout("new_k_sample", (n_attn, 2, WIN, 256))
    nv_s = dout("new_v_sample", (n_attn, 2, WIN, 256))
    ncv_s = dout("new_conv_sample", (max(n_conv, 1), 2, CP, D))

    s_w_in = [dscr("s_ffn1_w_in", (depth, D, 2 * FF)), dscr("s_ffn2_w_in", (depth, D, 2 * FF))]
    s_w_out = [dscr("s_ffn1_w_out", (depth, FF, D)), dscr("s_ffn2_w_out", (depth, FF, D))]
    s_qkv = dscr("s_qkv", (n_attn, D, 1536))
    s_wo = dscr("s_wo", (n_attn, D, D))
    s_cin = dscr("s_cin", (max(n_conv, 1), D, 2 * D))
    s_cout = dscr("s_cout", (max(n_conv, 1), D, D))
    s_diag = dscr("s_diag", (max(n_conv, 1), 128, NCH * CW * 128))

    st = ExitStack()
    with st:
        S = Sched(nc, st)
        st.enter_context(nc.allow_low_precision("bf16 matmul operands, fp32 accumulation"))

        def sb(name, shape, dt):
            return st.enter_context(nc.sbuf_tensor(name, list(shape), dt))

        xT = sb("xT", (128, NCH, 512), F32)
        b_x = [Buf("x%d" % c) for c in range(NCH)]
        hT = sb("hT", (128, NCH, 512), BF16)
        b_h = [Buf("h%d" % c) for c in range(NCH)]
        ringA = [sb("ringA%d" % i, (128, 8192), BF16) for i in range(3)]
        b_rA = [Buf("rA%d" % i) for i in range(3)]
        d_rA = [S.dsem() for _ in range(3)]
        d_rA_sw = [S.dsem() for _ in range(3)]
        ringB = [sb("ringB%d" % i, (128, 4096), BF16) for i in range(3)]
        b_rB = [Buf("rB%d" % i) for i in range(3)]
        d_rB = [S.dsem() for _ in range(3)]
        d_rB_sw = [S.dsem() for _ in range(3)]
        rA_i = [0]
        rB_i = [0]
        sg = [sb("sg%d" % i, (128, 512), BF16) for i in range(2)]
        b_sg = [Buf("sg%d" % i) for i in range(2)]
        rs = sb("rs", (128, 512), F32)
        b_rs = Buf("rs")
        ssum = sb("ssum", (128, 512), F32)
        b_ssum = Buf("ssum")
        ident = sb("ident", (128, 128), F32)
        identb = sb("identb", (128, 128), BF16)
        ones = sb("ones", (128, 128), F32)
        b_const = Buf("const")
        b_bias = Buf("biasT")
        b_btmp = Buf("btmp")
        PAR = sb("PAR", (128, NVEC * 8), F32)
        b_par = Buf("par")
        QKB = sb("QKB", (64, n_attn * 20), F32)
        KVB = sb("KVB", (128, n_attn, 512), F32)
        SNK = sb("SNK", (128, n_attn, NH, 1), F32)
        biasT = sb("biasT", (128, NH, 256), F32)
        kcar = [sb("kcar%d" % a, (64, NKV, WIN), BF16) for a in range(n_attn)]
        vcar = [sb("vcar%d" % a, (128, 256), BF16) for a in range(n_attn)]
        ucar = [sb("ucar%d" % c, (128, NCH, CP), BF16) for c in range(max(n_conv, 1))]
        b_kcar = [Buf("kcar%d" % a) for a in range(n_attn)]
        b_vcar = [Buf("vcar%d" % a) for a in range(n_attn)]
        b_ucar = [Buf("ucar%d" % c) for c in range(max(n_conv, 1))]
        small = sb("small", (128, 2, 12), F32)
        b_small = [Buf("small%d" % i) for i in range(2)]
        rtok = sb("rtok", (128, 4), F32)
        b_rtok = Buf("rtok")
        SCW = 16896
        SC = sb("SC", (128, SCW), F32)

        PS = st.enter_context(nc.psum_tensor("PS", [128, 4096], F32))
        banks = [PS[:, i * 512:(i + 1) * 512] for i in range(8)]
        b_bank = [Buf("bank%d" % i) for i in range(8)]

        def scv(off, words, dt=F32, parts=128):
            ap = SC[0:parts, off:off + words]
            return ap.bitcast(BF16) if dt == BF16 else ap

        sq = scv(0, 4096).rearrange("p (c n) -> p c n", c=NCH)
        aT = scv(4096, 5632, BF16).rearrange("p (j n) -> p j n", j=NJ)
        b_sq = [Buf("sq%d" % c) for c in range(NCH)]
        b_aT = [Buf("aT%d" % j) for j in range(NJ)]
        ffn_bufs = b_sq + b_aT
        qT = scv(0, 4096, BF16, 64).rearrange("p (h n) -> p h n", h=NH)
        oT = scv(4096, 4096, BF16, 64).rearrange("p (h n) -> p h n", h=NH)
        kTt = scv(8192, 1280, BF16, 64).rearrange("p (k n) -> p k n", k=NKV)
        Vt = scv(9472, 640, BF16).rearrange("p (b n) -> p b n", b=5)
        Sp = [scv(10112 + 1040 * i, 1040).rearrange("p (h n) -> p h n", h=4) for i in range(2)]
        Pn = [scv(12192 + 512 * i, 512, BF16).rearrange("p (h n) -> p h n", h=4) for i in range(2)]
        PTs = [scv(13216 + 512 * i, 512, BF16).rearrange("p (h n) -> p h n", h=4) for i in range(2)]
        kvtok = [scv(14240 + 512 * i, 512) for i in range(2)]
        kstage = scv(15264, 256)
        vstage = scv(15520, 256)
        junk = scv(15776, 128)
        b_qT = [Buf("qT%d" % h) for h in range(NH)]
        b_oT = [Buf("oT%d" % h) for h in range(NH)]
        b_kT = Buf("kT")
        b_V = Buf("V")
        b_Sp = [Buf("Sp%d" % i) for i in range(2)]
        b_Pn = [Buf("Pn%d" % i) for i in range(2)]
        b_PTs = [Buf("PTs%d" % i) for i in range(2)]
        b_kvtok = [Buf("kvtok%d" % i) for i in range(2)]
        b_kstage = Buf("kstage")
        b_vstage = Buf("vstage")
        b_junk = Buf("junk")
        attn_bufs = b_qT + b_oT + [b_kT, b_V] + b_Sp + b_Pn + b_PTs + b_kvtok + [b_kstage, b_vstage, b_junk]
        ub = scv(0, 2176, BF16).rearrange("p (c n) -> p c n", c=NCH)
        ust = scv(2176, 480).rearrange("p (c s t) -> p c s t", c=NCH, s=2)
        yT = scv(2688, 4096).rearrange("p (c n) -> p c n", c=NCH)
        zT = scv(6784, 2048, BF16).rearrange("p (c n) -> p c n", c=NCH)
        cst = [scv(8832 + 512 * i, 512) for i in range(4)]
        ysq = [scv(10880 + 512 * i, 512) for i in range(2)]
        sst = scv(11904, 1024)
        b_ub = [Buf("ub%d" % c) for c in range(NCH)]
        b_ust = Buf("ust")
        b_yT = [Buf("yT%d" % c) for c in range(NCH)]
        b_zT = [Buf("zT%d" % c) for c in range(NCH)]
        b_cst = [Buf("cst%d" % i) for i in range(4)]
        b_ysq = [Buf("ysq%d" % i) for i in range(2)]
        b_sst = Buf("sst")
        conv_bufs = b_ub + [b_ust] + b_yT + b_zT + b_cst + b_ysq + [b_sst]
        xin = scv(0, 4096).rearrange("p (k d) -> p k d", k=4)
        yfin = scv(4096, 4096).rearrange("p (c n) -> p c n", c=NCH)
        yout = scv(8192, 4096).rearrange("p (k d) -> p k d", k=4)
        cstage = scv(12288, 1024)
        b_xin = Buf("xin")
        b_yfin = [Buf("yfin%d" % c) for c in range(NCH)]
        b_yout = Buf("yout")
        b_cstage = Buf("cstage")
        io_bufs = [b_xin] + b_yfin + [b_yout, b_cstage]
        cur_view = [io_bufs]

        def switch(view):
            if cur_view[0] is not view:
                fence(cur_view[0], view)
                cur_view[0] = view

        out_bufs = []
        d_outs = {"y": S.dsem(), "kv": S.dsem(), "cv": S.dsem()}
        d_misc = S.dsem()
        d_misc_sw = S.dsem()
        d_xin = S.dsem()

        def out_dma(kind, fn, reads):
            b = Buf("out%d" % len(out_bufs))
            out_bufs.append(b)
            S.op("pool", fn, reads=reads, writes=[b], dsem=d_outs[kind])

        S.op("pool", lambda h: h.memset(ones[:], 1.0), writes=[b_const])
        S.op("pool", lambda h: h.memset(ident[:], 1.0), writes=[b_const])
        S.op("pool", lambda h: h.affine_select(out=ident[:], in_=ident[:], pattern=[[-1, 128]], compare_op=ALU.is_equal,
                                                fill=0.0, base=0, channel_multiplier=1), reads=[b_const], writes=[b_const])
        S.op("pool", lambda h: h.tensor_copy(out=identb[:], in_=ident[:]), reads=[b_const], writes=[b_const])
        ngrp = (NVEC * 8 + 127) // 128
        for g in range(ngrp):
            r0 = g * 128
            nr = min(128, NVEC * 8 - r0)
            S.op("sp", lambda h: h.dma_start(out=SC[0:nr, 512:640], in_=vecs[r0:r0 + nr, :]), writes=io_bufs, dsem=d_misc)
            S.op("pe", lambda h: h.transpose(out=banks[0][:, 0:nr], in_=SC[0:nr, 512:640], identity=ident[0:nr, 0:nr]),
                 reads=io_bufs + [b_const], writes=[b_bank[0]])
            S.op("dve", lambda h: h.tensor_copy(out=PAR[:, r0:r0 + nr], in_=banks[0][:, 0:nr]), reads=[b_bank[0]], writes=[b_par])
        nr = n_attn * 20
        S.op("sp", lambda h: h.dma_start(out=SC[0:nr, 512:576], in_=qkb[:, :]), writes=io_bufs, dsem=d_misc)
        S.op("pe", lambda h: h.transpose(out=banks[0][0:64, 0:nr], in_=SC[0:nr, 512:576], identity=ident[0:nr, 0:nr]),
             reads=io_bufs + [b_const], writes=[b_bank[0]])
        S.op("dve", lambda h: h.tensor_copy(out=QKB[:, :], in_=banks[0][0:64, 0:nr]), reads=[b_bank[0]], writes=[b_par])
        for a in range(n_attn):
            S.op("sp", lambda h: [h.dma_start(out=KVB[:, a, :], in_=kvb[a].partition_broadcast(128)),
                                  h.dma_start(out=SNK[:, a, :, 0], in_=sinks[a].partition_broadcast(128))],
                 writes=[b_par], dsem=d_misc)

        S.op("sp", lambda h: h.dma_start(out=biasT[:, :, :].rearrange("p h n -> p (h n)"), in_=alibi[:, :]), writes=[b_bias], dsem=S.dsem())
        def pcol(vec, c):
            return PAR[:, vec * 8 + c: vec * 8 + c + 1]
        V_N1, V_NM, V_N2, V_NF = 0, depth, 2 * depth, 3 * depth
        V_CBIN = 3 * depth + 1
        V_CBDW = V_CBIN + 2 * max(n_conv, 1)
        V_LNG = V_CBDW + max(n_conv, 1)
        V_LNB = V_LNG + max(n_conv, 1)
        V_CBO = V_LNB + max(n_conv, 1)
        V_WDW = V_CBO + max(n_conv, 1)

        b_sw = {}

        pending = []

        def precast(key, dst, src, rows, cols, after=()):
            b = Buf("sw_" + str(key))
            ds = S.dsem()
            b_sw[key] = b
            split = 1
            while cols // split > 2048:
                split *= 2
            step = 256
            for r in range(0, rows, step):
                r1 = min(rows, r + step)

                def chunk(r=r, r1=r1):
                    tok = S.op("pool", lambda h: h.dma_start(out=dst[r:r1, :].rearrange("r (s n) -> r s n", s=split),
                                                             in_=src[r:r1, :].rearrange("r (s n) -> r s n", s=split)),
                               writes=[Buf("tmp")], dsem=ds)
                    b.w = {id(tok[0]): tok}
                pending.append(chunk)

        def trickle(k=1):
            for _ in range(k):
                if pending:
                    pending.pop(0)()

        def emit_precast(l, after=()):
            precast(("in", 0, l), s_w_in[0][l], ffn_w_in[0][l], D, 2 * FF, after)
            precast(("out", 0, l), s_w_out[0][l], ffn_w_out[0][l], FF, D, after)
            if l % 2 == 0:
                a = l // 2
                precast(("qkv", a), s_qkv[a], attn_w_qkv[a], D, 1536, after)
                precast(("wo", a), s_wo[a], attn_w_o[a], D, D, after)
            else:
                c = l // 2
                precast(("cin", c), s_cin[c], conv_w_in[c], D, 2 * D, after)
                precast(("cout", c), s_cout[c], conv_w_out[c], D, D, after)
            precast(("in", 1, l), s_w_in[1][l], ffn_w_in[1][l], D, 2 * FF, after)
            precast(("out", 1, l), s_w_out[1][l], ffn_w_out[1][l], FF, D, after)

        diag_building = [False]
        d_diag = {}
        cur_tile = [0]

        def wsel(key, layer, fp32_ap, bf16_ap):
            if cur_tile[0] <= layer:
                return fp32_ap, "pool", []
            return bf16_ap, "sp", [b_sw[key]]

        def loadA(fn_dst_src, eng, deps):
            i = rA_i[0] % 3
            rA_i[0] += 1
            S.op(eng, lambda h: fn_dst_src(h, ringA[i]), reads=deps, writes=[b_rA[i]], dsem=(d_rA_sw[i] if eng == "pool" else d_rA[i]))
            if rA_i[0] % 2 == 0:
                trickle()
            return i

        def loadB(fn_dst_src, eng, deps):
            i = rB_i[0] % 3
            rB_i[0] += 1
            S.op(eng, lambda h: fn_dst_src(h, ringB[i]), reads=deps, writes=[b_rB[i]], dsem=(d_rB_sw[i] if eng == "pool" else d_rB[i]))
            if rB_i[0] % 2 == 0:
                trickle()
            return i

        def prenorm(n, gvec):
            switch(ffn_bufs)
            for c in range(NCH):
                S.op("act", lambda h: h.activation(out=sq[:, c, :n], in_=xT[:, c, :n], func=AF.Square),
                     reads=[b_x[c]], writes=[b_sq[c]])
            for c in range(NCH):
                S.op("dve", lambda h: h.tensor_scalar(out=hT[:, c, :n], in0=xT[:, c, :n], scalar1=pcol(gvec, c), scalar2=None, op0=ALU.mult),
                     reads=[b_x[c], b_par], writes=[b_h[c]])
            S.op("dve", lambda h: h.tensor_tensor(out=ssum[:, :n], in0=sq[:, 0, :n], in1=sq[:, 1, :n], op=ALU.add),
                 reads=[b_sq[0], b_sq[1]], writes=[b_ssum])
            for c in range(2, NCH):
                S.op("dve", lambda h: h.tensor_tensor(out=ssum[:, :n], in0=ssum[:, :n], in1=sq[:, c, :n], op=ALU.add),
                     reads=[b_sq[c]], writes=[b_ssum])

            def stats():
                S.op("pe", lambda h: h.matmul(banks[7][:, :n], lhsT=ones[:], rhs=ssum[:, :n], start=True, stop=True),
                     reads=[b_ssum, b_const], writes=[b_bank[7]])
                S.op("act", lambda h: h.activation(out=rs[:, :n], in_=banks[7][:, :n], func=AF.Sqrt, scale=1.0 / D, bias=EPS),
                     reads=[b_bank[7]], writes=[b_rs])
                S.op("dve", lambda h: h.reciprocal(out=rs[:, :n], in_=rs[:, :n]), reads=[b_rs], writes=[b_rs])
            return stats

        def ffn(n, which, l, gvec):
            stats = prenorm(n, gvec)
            gsb = [sq[:, 4, :], sq[:, 5, :]]
            usb = [sq[:, 6, :], sq[:, 7, :]]
            b_gsb = [b_sq[4], b_sq[5]]
            b_usb = [b_sq[6], b_sq[7]]
            src_in, eng_in, dep_in = wsel(("in", which, l), l, ffn_w_in[which][l], s_w_in[which][l])
            src_out, eng_out, dep_out = wsel(("out", which, l), l, ffn_w_out[which][l], s_w_out[which][l])
            w_in = src_in.rearrange("(c p) (gu f) -> p c gu f", p=128, gu=2)
            w_out = src_out.rearrange("(j p) d -> p j d", p=128)

            def evacA(k, j):
                gb, ub = k % 3, 3 + k % 3
                S.op("dve", lambda h: h.tensor_tensor(out=gsb[k % 2][:, :n], in0=banks[gb][:, :n], in1=rs[:, :n], op=ALU.mult),
                     reads=[b_bank[gb], b_rs], writes=[b_gsb[k % 2]])
                S.op("act", lambda h: h.activation(out=sg[k % 2][:, :n], in_=gsb[k % 2][:, :n], func=AF.Silu),
                     reads=[b_gsb[k % 2]], writes=[b_sg[k % 2]])
                S.op("dve", lambda h: h.tensor_tensor(out=usb[k % 2][:, :n], in0=banks[ub][:, :n], in1=rs[:, :n], op=ALU.mult),
                     reads=[b_bank[ub], b_rs], writes=[b_usb[k % 2]])
                S.op("dve", lambda h: h.tensor_tensor(out=aT[:, j, :n], in0=usb[k % 2][:, :n], in1=sg[k % 2][:, :n], op=ALU.mult),
                     reads=[b_usb[k % 2], b_sg[k % 2]], writes=[b_aT[j]])

            k = 0
            for fg in range(6):
                j0 = fg * 4
                nj = min(4, NJ - j0)
                i = loadA(lambda h, dst: [h.dma_start(
                    out=dst[:, 0:NCH * 2 * nj * 128].rearrange("p (c gu f) -> p c gu f", c=NCH, gu=2)[:, :, gu, :],
                    in_=w_in[:, :, gu, j0 * 128:(j0 + nj) * 128]) for gu in range(2)], eng_in, dep_in)
                wv = ringA[i][:, 0:NCH * 2 * nj * 128].rearrange("p (c gu f) -> p c gu f", c=NCH, gu=2)
                for jj in range(nj):
                    j = j0 + jj
                    gb, ub = k % 3, 3 + k % 3
                    S.op("pe", lambda h: [h.matmul(banks[gb][:, :n], lhsT=wv[:, c, 0, jj * 128:(jj + 1) * 128], rhs=hT[:, c, :n],
                                                   start=(c == 0), stop=(c == NCH - 1)) for c in range(NCH)],
                         reads=[b_rA[i]] + b_h, writes=[b_bank[gb]])
                    S.op("pe", lambda h: [h.matmul(banks[ub][:, :n], lhsT=wv[:, c, 1, jj * 128:(jj + 1) * 128], rhs=hT[:, c, :n],
                                                   start=(c == 0), stop=(c == NCH - 1)) for c in range(NCH)],
                         reads=[b_rA[i]] + b_h, writes=[b_bank[ub]])
                    if k == 1:
                        stats()
                        evacA(0, 0)
                        evacA(1, 1)
                    elif k > 1:
                        evacA(k, j)
                    k += 1
            for half in range(2):
                bset = [4, 5, 6, 7] if half == 0 else [0, 1, 2, 3]
                for fgB in range(3):
                    j0 = fgB * 8
                    nj = min(8, NJ - j0)
                    i = loadB(lambda h, dst: h.dma_start(
                        out=dst[:, 0:nj * 512].rearrange("p (j d) -> p j d", j=nj),
                        in_=w_out[:, j0:j0 + nj, half * 512:(half + 1) * 512]), eng_out, dep_out)
                    wv = ringB[i][:, 0:nj * 512].rearrange("p (j d) -> p j d", j=nj)
                    S.op("pe", lambda h: [h.matmul(banks[bset[dd]][:, :n], lhsT=wv[:, jj, dd * 128:(dd + 1) * 128], rhs=aT[:, j0 + jj, :n],
                                                   start=(j0 + jj == 0), stop=(j0 + jj == NJ - 1))
                                          for jj in range(nj) for dd in range(4)],
                         reads=[b_rB[i]] + b_aT[j0:j0 + nj], writes=[b_bank[b] for b in bset])
                for dd in range(4):
                    c = half * 4 + dd
                    S.op("dve", lambda h: h.scalar_tensor_tensor(out=xT[:, c, :n], in0=banks[bset[dd]][:, :n], scalar=0.5,
                                                                 in1=xT[:, c, :n], op0=ALU.mult, op1=ALU.add),
                         reads=[b_bank[bset[dd]]], writes=[b_x[c]])

        ev = [0]

        def evac(out, in_, reads, writes, bias=None):
            ev[0] += 1
            if ev[0] % 2 == 0:
                if bias is None:
                    S.op("act", lambda h: h.copy(out=out, in_=in_), reads=reads, writes=writes)
                else:
                    S.op("act", lambda h: h.activation(out=out, in_=in_, func=AF.Identity, bias=bias, scale=1.0),
                         reads=reads + [b_par], writes=writes)
            else:
                if bias is None:
                    S.op("dve", lambda h: h.tensor_copy(out=out, in_=in_), reads=reads, writes=writes)
                else:
                    S.op("dve", lambda h: h.tensor_scalar(out=out, in0=in_, scalar1=bias, scalar2=None, op0=ALU.add),
                         reads=reads + [b_par], writes=writes)

        def attn_groups(a, groups):
            G = len(groups)

            def stA1(g):
                gr = groups[g]; r = g % 2
                nq, nkeys, h0 = gr["nq"], gr["nkeys"], gr["h0"]
                scb = PS[:, r * 1024:(r + 1) * 1024].rearrange("p (h n) -> p h n", h=4)
                bks = [b_bank[2 * r], b_bank[2 * r + 1]]
                S.op("pe", lambda h: [h.matmul(scb[0:nq, hh, 0:nkeys], lhsT=gr["q_ap"](h0 + hh), rhs=gr["k_ap"], start=True, stop=True) for hh in range(4)],
                     reads=b_qT[h0:h0 + 4] + [b_kT], writes=bks)
                S.op("dve", lambda h: h.scalar_tensor_tensor(out=Sp[r][0:nq, :, 0:nkeys], in0=scb[0:nq, :, 0:nkeys], scalar=HD ** -0.5,
                                                             in1=biasT[0:nq, h0:h0 + 4, 0:nkeys], op0=ALU.mult, op1=ALU.add),
                     reads=bks + [b_bias], writes=[b_Sp[r]])
                if gr["mask_first"]:
                    S.op("dve", lambda h: h.memset(Sp[r][0:nq, :, 0:128], NEG), reads=[b_Sp[r]], writes=[b_Sp[r]])

            def stA2(g):
                gr = groups[g]; r = g % 2
                nq, nkeys, h0 = gr["nq"], gr["nkeys"], gr["h0"]
                S.op("act", lambda h: h.copy(out=Sp[r][0:nq, :, nkeys:nkeys + 1], in_=SNK[0:nq, a, h0:h0 + 4, :]),
                     reads=[b_Sp[r], b_par], writes=[b_Sp[r]])

            def stA3(g):
                gr = groups[g]; r = g % 2
                nq, nkeys = gr["nq"], gr["nkeys"]
                S.op("dve", lambda h: h.tensor_reduce(out=small[0:nq, r, 0:4], in_=Sp[r][0:nq, :, 0:nkeys + 1], axis=AX.X, op=ALU.max, negate=True),
                     reads=[b_Sp[r]], writes=[b_small[r]])

            def stB1(g):
                gr = groups[g]; r = g % 2
                nq, nkeys = gr["nq"], gr["nkeys"]
                for hh in range(4):
                    S.op("act", lambda h: h.activation(out=Sp[r][0:nq, hh, 0:nkeys + 1], in_=Sp[r][0:nq, hh, 0:nkeys + 1], func=AF.Exp,
                                                       bias=small[0:nq, r, hh:hh + 1], scale=1.0, accum_out=small[0:nq, r, 4 + hh:5 + hh]),
                         reads=[b_Sp[r], b_small[r]], writes=[b_Sp[r], b_small[r]])

            def stB2(g):
                gr = groups[g]; r = g % 2
                nq, nkeys = gr["nq"], gr["nkeys"]
                S.op("dve", lambda h: h.reciprocal(out=small[0:nq, r, 8:12], in_=small[0:nq, r, 4:8]), reads=[b_small[r]], writes=[b_small[r]])
                S.op("dve", lambda h: h.tensor_tensor(out=Pn[r][0:nq, :, 0:nkeys], in0=Sp[r][0:nq, :, 0:nkeys],
                                                      in1=small[0:nq, r, 8:12].unsqueeze(2).to_broadcast([nq, 4, nkeys]), op=ALU.mult),
                     reads=[b_Sp[r], b_small[r]], writes=[b_Pn[r]])

            def stC(g):
                gr = groups[g]; r = g % 2
                nq = gr["nq"]
                ptb = banks[4 + r].bitcast(BF16).rearrange("p (h n) -> p h n", h=4)
                k0s = []
                k0 = 0
                for (v_ap, nk) in gr["vblocks"]:
                    k0s.append(k0)
                    k0 += nk
                S.op("pe", lambda h: [h.transpose(out=ptb[0:nk, hh, bi * 128:bi * 128 + nq], in_=Pn[r][0:nq, hh, k0s[bi]:k0s[bi] + nk],
                                                  identity=identb[0:nq, 0:nq]) for hh in range(4) for bi, (v_ap, nk) in enumerate(gr["vblocks"])],
                     reads=[b_Pn[r], b_const], writes=[b_bank[4 + r]])
                for bi, (v_ap, nk) in enumerate(gr["vblocks"]):
                    evac(PTs[r][0:nk, :, bi * 128:bi * 128 + nq], ptb[0:nk, :, bi * 128:bi * 128 + nq], [b_bank[4 + r]], [b_PTs[r]])

            def stD(g):
                gr = groups[g]; r = g % 2
                nq, h0 = gr["nq"], gr["h0"]
                ob = banks[6 + r][0:64, :].rearrange("p (h n) -> p h n", h=4)
                nb = len(gr["vblocks"])
                S.op("pe", lambda h: [h.matmul(ob[:, hh, 0:nq], lhsT=v_ap, rhs=PTs[r][0:nk, hh, bi * 128:bi * 128 + nq],
                                               start=(bi == 0), stop=(bi == nb - 1)) for hh in range(4) for bi, (v_ap, nk) in enumerate(gr["vblocks"])],
                     reads=[b_PTs[r], b_V], writes=[b_bank[6 + r]])
                evac(gr["o_ap"], ob[:, :, 0:nq], [b_bank[6 + r]], b_oT[h0:h0 + 4])

            for step in range(G + 3):
                if step < G:
                    stA1(step)
                if 0 <= step - 1 < G:
                    stB1(step - 1)
                if step < G:
                    stA2(step)
                if 0 <= step - 1 < G:
                    stB2(step - 1)
                if step < G:
                    stA3(step)
                if 0 <= step - 2 < G:
                    stC(step - 2)
                if 0 <= step - 3 < G:
                    stD(step - 3)

        def attn(n, l, prompt, b_idx, t_idx, stats):
            a = l // 2
            switch(attn_bufs)
            rmsnorm_attn = None
            src_q, eng_q, dep_q = wsel(("qkv", a), l, attn_w_qkv[a], s_qkv[a])
            wq = src_q.rearrange("(c p) n -> p c n", p=128)
            iq = loadA(lambda h, dst: h.dma_start(out=dst[:, :].rearrange("p (c n) -> p c n", c=NCH), in_=wq[:, :, 0:1024]), eng_q, dep_q)
            ikv = loadA(lambda h, dst: h.dma_start(out=dst[:, 0:4096].rearrange("p (c n) -> p c n", c=NCH), in_=wq[:, :, 1024:1536]), eng_q, dep_q)
            wqv = ringA[iq][:, :].rearrange("p (c n) -> p c n", c=NCH)
            wkv = ringA[ikv][:, 0:4096].rearrange("p (c n) -> p c n", c=NCH)
            if prompt:
                if t_idx == 0:
                    S.op("dve", lambda h: h.memset(kTt[:, :, 0:128], 0.0), writes=[b_kT])
                    S.op("dve", lambda h: h.memset(Vt[:, 0, :], 0.0), writes=[b_V])
                else:
                    S.op("dve", lambda h: h.tensor_copy(out=kTt[:, :, 0:128], in_=kcar[a][:, :, :]), reads=[b_kcar[a]], writes=[b_kT])
                    S.op("dve", lambda h: h.tensor_copy(out=Vt[:, 0, :], in_=vcar[a][:, :]), reads=[b_vcar[a]], writes=[b_V])
            def evacQ(hh):
                bk = hh % 4
                S.op("dve", lambda h: h.tensor_tensor(out=kvtok[hh % 2][0:64, :n], in0=banks[bk][0:64, :n], in1=rs[0:64, :n], op=ALU.mult),
                     reads=[b_bank[bk], b_rs], writes=[b_kvtok[hh % 2]])
                S.op("act", lambda h: h.activation(out=qT[:, hh, :n], in_=kvtok[hh % 2][0:64, :n], func=AF.Identity,
                                                   bias=QKB[:, a * 20 + hh:a * 20 + hh + 1], scale=1.0),
                     reads=[b_kvtok[hh % 2], b_par], writes=[b_qT[hh]])

            for hh in range(NH):
                bk = hh % 4
                S.op("pe", lambda h: [h.matmul(banks[bk][0:64, :n], lhsT=wqv[:, c, hh * 64:(hh + 1) * 64], rhs=hT[:, c, :n],
                                               start=(c == 0), stop=(c == NCH - 1)) for c in range(NCH)],
                     reads=[b_rA[iq]] + b_h, writes=[b_bank[bk]])
                if hh == 1:
                    stats()
                    evacQ(0)
                    evacQ(1)
                elif hh > 1:
                    evacQ(hh)
            for kv in range(NKV):
                bk = kv % 4
                S.op("pe", lambda h: [h.matmul(banks[bk][0:64, :n], lhsT=wkv[:, c, kv * 64:(kv + 1) * 64], rhs=hT[:, c, :n],
                                               start=(c == 0), stop=(c == NCH - 1)) for c in range(NCH)],
                     reads=[b_rA[ikv]] + b_h, writes=[b_bank[bk]])
                S.op("dve", lambda h: h.tensor_tensor(out=kvtok[kv % 2][0:64, :n], in0=banks[bk][0:64, :n], in1=rs[0:64, :n], op=ALU.mult),
                     reads=[b_bank[bk], b_rs], writes=[b_kvtok[kv % 2]])
                S.op("act", lambda h: h.activation(out=kTt[:, kv, 128:128 + n], in_=kvtok[kv % 2][0:64, :n], func=AF.Identity,
                                                   bias=QKB[:, a * 20 + 16 + kv:a * 20 + 17 + kv], scale=1.0),
                     reads=[b_kvtok[kv % 2], b_par], writes=[b_kT])
            if prompt:
                blocks = [(blk * 128, 128) for blk in range(n // 128)]
            else:
                blocks = [(sb_ * dseq, dseq) for sb_ in range(2)]
            for bi, (t0, nt) in enumerate(blocks):
                bk = 4 + bi % 2
                kk = bi % 2
                S.op("pe", lambda h: [h.matmul(banks[bk][0:nt, :], lhsT=hT[:, c, t0:t0 + nt], rhs=wkv[:, c, :],
                                               start=(c == 0), stop=(c == NCH - 1)) for c in range(NCH)],
                     reads=[b_rA[ikv]] + b_h, writes=[b_bank[bk]])
                S.op("dve", lambda h: h.scalar_tensor_tensor(out=junk[0:nt, 0:nt], in0=rs[0:nt, t0:t0 + nt], scalar=1.0, in1=ident[0:nt, 0:nt],
                                                             op0=ALU.mult, op1=ALU.mult, accum_out=rtok[0:nt, bi:bi + 1]),
                     reads=[b_rs, b_const], writes=[b_junk, b_rtok])
                S.op("dve", lambda h: h.scalar_tensor_tensor(out=kvtok[kk][0:nt, :], in0=banks[bk][0:nt, :], scalar=rtok[0:nt, bi:bi + 1],
                                                             in1=KVB[0:nt, a, :], op0=ALU.mult, op1=ALU.add),
                     reads=[b_bank[bk], b_par, b_rtok], writes=[b_kvtok[kk]])
                if prompt:
                    S.op("act", lambda h: h.copy(out=Vt[:, 1 + bi, :], in_=kvtok[kk][:, 256:512]), reads=[b_kvtok[kk]], writes=[b_V])
                    if t_idx == NT - 1 and bi == len(blocks) - 1:
                        out_dma("kv", lambda h: [h.dma_start(out=nk_p[a, b_idx, :, :], in_=kvtok[kk][:, 0:256]),
                                                 h.dma_start(out=nv_p[a, b_idx, :, :], in_=kvtok[kk][:, 256:512])],
                                [b_kvtok[kk]])
                else:
                    S.op("act", lambda h: h.copy(out=Vt[0:nt, 1 + bi, :], in_=kvtok[kk][0:nt, 256:512]), reads=[b_kvtok[kk]], writes=[b_V])
                    out_dma("kv", lambda h: [h.dma_start(out=nk_s[a, bi, WIN - dseq:WIN, :], in_=kvtok[kk][0:nt, 0:256]),
                                             h.dma_start(out=nv_s[a, bi, WIN - dseq:WIN, :], in_=kvtok[kk][0:nt, 256:512]),
                                             h.dma_start(out=nk_s[a, bi, 0:WIN - dseq, :], in_=cache_k[a, bi, dseq:WIN, :]),
                                             h.dma_start(out=nv_s[a, bi, 0:WIN - dseq, :], in_=cache_v[a, bi, dseq:WIN, :])],
                            [b_kvtok[kk]])
            if prompt:
                if t_idx < NT - 1:
                    S.op("act", lambda h: h.copy(out=kcar[a][:, :, :], in_=kTt[:, :, 512:640]), reads=[b_kT], writes=[b_kcar[a]])
                    S.op("act", lambda h: h.copy(out=vcar[a][:, :], in_=Vt[:, 4, :]), reads=[b_V], writes=[b_vcar[a]])
                groups = []
                for blk in range(n // 128):
                    for kv in range(NKV):
                        groups.append(dict(nq=128, h0=kv * 4, q_ap=(lambda hh, blk=blk: qT[:, hh, blk * 128:(blk + 1) * 128]),
                                           k_ap=kTt[:, kv, blk * 128:blk * 128 + 256], nkeys=256,
                                           vblocks=[(Vt[:, blk, kv * 64:(kv + 1) * 64], 128), (Vt[:, blk + 1, kv * 64:(kv + 1) * 64], 128)],
                                           o_ap=oT[:, kv * 4:kv * 4 + 4, blk * 128:(blk + 1) * 128], mask_first=(t_idx == 0 and blk == 0)))
                attn_groups(a, groups)
            else:
                groups = []
                for sb_ in range(2):
                    kc0 = 256 + 192 * sb_
                    vblk = 0 if sb_ == 0 else 3
                    S.op("pool", lambda h: [h.dma_start(out=kstage[:, :], in_=cache_k[a, sb_, :, :]),
                                            h.dma_start(out=vstage[:, :], in_=cache_v[a, sb_, :, :])],
                         writes=[b_kstage, b_vstage], dsem=d_misc_sw)
                    for kv in range(NKV):
                        S.op("pe", lambda h: h.transpose(out=banks[6][0:64, kv * 128:(kv + 1) * 128], in_=kstage[:, kv * 64:(kv + 1) * 64], identity=ident[:]),
                             reads=[b_kstage, b_const], writes=[b_bank[6]])
                    S.op("dve", lambda h: h.tensor_copy(out=kTt[:, :, kc0:kc0 + 128], in_=banks[6][0:64, :].rearrange("p (k n) -> p k n", k=NKV)),
                         reads=[b_bank[6]], writes=[b_kT])
                    S.op("dve", lambda h: h.tensor_copy(out=kTt[:, :, kc0 + 128:kc0 + 128 + dseq], in_=kTt[:, :, 128 + sb_ * dseq:128 + (sb_ + 1) * dseq]),
                         reads=[b_kT], writes=[b_kT])
                    S.op("act", lambda h: h.copy(out=Vt[:, vblk, :], in_=vstage[:, :]), reads=[b_vstage], writes=[b_V])
                    for kv in range(NKV):
                        groups.append(dict(nq=dseq, h0=kv * 4, q_ap=(lambda hh, sb_=sb_: qT[:, hh, sb_ * dseq:(sb_ + 1) * dseq]),
                                           k_ap=kTt[:, kv, kc0:kc0 + 128 + dseq], nkeys=WIN + dseq,
                                           vblocks=[(Vt[:, vblk, kv * 64:(kv + 1) * 64], 128), (Vt[0:dseq, 1 + sb_, kv * 64:(kv + 1) * 64], dseq)],
                                           o_ap=oT[:, kv * 4:kv * 4 + 4, sb_ * dseq:(sb_ + 1) * dseq], mask_first=False))
                attn_groups(a, groups)
            src_o, eng_o, dep_o = wsel(("wo", a), l, attn_w_o[a], s_wo[a])
            wo = src_o.rearrange("(hh p) n -> p hh n", p=64)
            for half in range(2):
                i = loadA(lambda h, dst: h.dma_start(out=dst[0:64, :].rearrange("p (hh n) -> p hh n", hh=NH),
                                                     in_=wo[:, :, half * 512:(half + 1) * 512]), eng_o, dep_o)
                wv = ringA[i][0:64, :].rearrange("p (hh n) -> p hh n", hh=NH)
                for dd in range(4):
                    c = half * 4 + dd
                    bk = 6 + dd % 2
                    S.op("pe", lambda h: [h.matmul(banks[bk][:, :n], lhsT=wv[:, hh, dd * 128:(dd + 1) * 128], rhs=oT[:, hh, :n],
                                                   start=(hh == 0), stop=(hh == NH - 1)) for hh in range(NH)],
                         reads=[b_rA[i]] + b_oT, writes=[b_bank[bk]])
                    S.op("dve", lambda h: h.tensor_tensor(out=xT[:, c, :n], in0=banks[bk][:, :n], in1=xT[:, c, :n], op=ALU.add),
                         reads=[b_bank[bk]], writes=[b_x[c]])

        def conv(n, l, prompt, b_idx, t_idx, stats):
            cl = l // 2
            switch(conv_bufs)
            nseq = 1 if prompt else 2
            L = n // nseq
            UL = CP + L
            ubv = ub[:, :, 0:nseq * UL].rearrange("p c (s u) -> p c s u", s=nseq)
            need_state = (not prompt) or t_idx == NT - 1

            def tokv(ap2):
                return ap2.rearrange("p (s t) -> p s t", s=nseq)
            src_ci, eng_ci, dep_ci = wsel(("cin", cl), l, conv_w_in[cl], s_cin[cl])
            win = src_ci.rearrange("(c p) (ag f) -> p c ag f", p=128, ag=2)
            if prompt:
                if t_idx == 0:
                    S.op("dve", lambda h: h.memset(ubv[:, :, :, 0:CP], 0.0), writes=b_ub)
                else:
                    S.op("dve", lambda h: h.tensor_copy(out=ubv[:, :, 0, 0:CP], in_=ucar[cl][:, :, :]), reads=[b_ucar[cl]], writes=b_ub)
            else:
                for sb_ in range(2):
                    S.op("pool", lambda h: h.dma_start(out=sst[0:CP, :], in_=state_conv[cl, sb_, :, :]), writes=[b_sst], dsem=d_misc_sw)
                    for half in range(2):
                        S.op("pe", lambda h: [h.transpose(out=banks[6 + half][:, jj * 32:jj * 32 + CP], in_=sst[0:CP, (half * 4 + jj) * 128:(half * 4 + jj + 1) * 128],
                                                          identity=ident[0:CP, 0:CP]) for jj in range(4)],
                             reads=[b_sst, b_const], writes=[b_bank[6 + half]])
                        S.op("dve", lambda h: h.tensor_copy(out=ubv[:, half * 4:half * 4 + 4, sb_, 0:CP],
                                                            in_=banks[6 + half][:, 0:128].rearrange("p (j t) -> p j t", j=4)[:, :, 0:CP]),
                             reads=[b_bank[6 + half]], writes=b_ub[half * 4:half * 4 + 4])
            def evacG(k, j):
                ab, gb = k % 3, 3 + k % 3
                sgm = cst[2 + k % 2]
                av = ysq[k % 2]
                S.op("dve", lambda h: h.tensor_tensor(out=sgm[:, :n], in0=banks[gb][:, :n], in1=rs[:, :n], op=ALU.mult),
                     reads=[b_bank[gb], b_rs], writes=[b_cst[2 + k % 2]])
                S.op("act", lambda h: h.activation(out=sgm[:, :n], in_=sgm[:, :n], func=AF.Sigmoid,
                                                   bias=pcol(V_CBIN + 2 * cl + 1, j), scale=1.0),
                     reads=[b_par], writes=[b_cst[2 + k % 2]])
                S.op("dve", lambda h: h.tensor_tensor(out=av[:, :n], in0=banks[ab][:, :n], in1=rs[:, :n], op=ALU.mult),
                     reads=[b_bank[ab], b_rs], writes=[b_ysq[k % 2]])
                S.op("dve", lambda h: h.scalar_tensor_tensor(out=ubv[:, j, :, CP:UL], in0=tokv(av[:, :n]),
                                                             scalar=pcol(V_CBIN + 2 * cl, j), in1=tokv(sgm[:, :n]),
                                                             op0=ALU.add, op1=ALU.mult),
                     reads=[b_ysq[k % 2], b_cst[2 + k % 2], b_par], writes=[b_ub[j]])
                if need_state:
                    S.op("dve", lambda h: h.scalar_tensor_tensor(out=ust[:, j, 0:nseq, :], in0=tokv(av[:, :n])[:, :, L - CP:L],
                                                                 scalar=pcol(V_CBIN + 2 * cl, j), in1=tokv(sgm[:, :n])[:, :, L - CP:L],
                                                                 op0=ALU.add, op1=ALU.mult),
                         reads=[b_ysq[k % 2], b_cst[2 + k % 2], b_par], writes=[b_ust])

            k = 0
            for piece in range(2):
                i = loadA(lambda h, dst: [h.dma_start(out=dst[:, :].rearrange("p (c ag f) -> p c ag f", c=NCH, ag=2)[:, :, ag, :],
                                                      in_=win[:, :, ag, piece * 512:(piece + 1) * 512]) for ag in range(2)], eng_ci, dep_ci)
                wv = ringA[i][:, :].rearrange("p (c ag f) -> p c ag f", c=NCH, ag=2)
                for jj in range(4):
                    j = piece * 4 + jj
                    ab, gb = k % 3, 3 + k % 3
                    S.op("pe", lambda h: [h.matmul(banks[ab][:, :n], lhsT=wv[:, c, 0, jj * 128:(jj + 1) * 128], rhs=hT[:, c, :n],
                                                   start=(c == 0), stop=(c == NCH - 1)) for c in range(NCH)],
                         reads=[b_rA[i]] + b_h, writes=[b_bank[ab]])
                    S.op("pe", lambda h: [h.matmul(banks[gb][:, :n], lhsT=wv[:, c, 1, jj * 128:(jj + 1) * 128], rhs=hT[:, c, :n],
                                                   start=(c == 0), stop=(c == NCH - 1)) for c in range(NCH)],
                         reads=[b_rA[i]] + b_h, writes=[b_bank[gb]])
                    if k == 1:
                        stats()
                        evacG(0, 0)
                        evacG(1, 1)
                    elif k > 1:
                        evacG(k, j)
                    k += 1
            if prompt and t_idx < NT - 1:
                S.op("act", lambda h: h.copy(out=ucar[cl][:, :, :], in_=ubv[:, :, 0, L:UL]), reads=b_ub, writes=[b_ucar[cl]])
            if need_state:
                for sb_ in range(nseq):
                    for half in range(2):
                        S.op("pe", lambda h: [h.transpose(out=banks[6 + half][0:CP, jj * 128:(jj + 1) * 128], in_=ust[:, half * 4 + jj, sb_, :],
                                                          identity=ident[:]) for jj in range(4)],
                             reads=[b_ust, b_const], writes=[b_bank[6 + half]])
                        S.op("act", lambda h: h.copy(out=sst[0:CP, half * 512:(half + 1) * 512], in_=banks[6 + half][0:CP, :]),
                             reads=[b_bank[6 + half]], writes=[b_sst])
                    dst = ncv_p[cl, b_idx, :, :] if prompt else ncv_s[cl, sb_, :, :]
                    out_dma("cv", lambda h: h.dma_start(out=dst, in_=sst[0:CP, :]), [b_sst])
            for pr in range(4):
                if ("diag", cl) in b_sw and not diag_building[0]:
                    i = loadA(lambda h, dst: h.dma_start(out=dst[:, 0:2 * CW * 128], in_=s_diag[cl][:, pr * 2 * CW * 128:(pr + 1) * 2 * CW * 128]),
                              "sp", [b_sw[("diag", cl)]])
                else:
                    if pr == 0:
                        diag_building[0] = True
                        b_sw[("diag", cl)] = Buf("sw_diag%d" % cl)
                        d_diag[cl] = S.dsem()
                    i = rA_i[0] % 3
                    rA_i[0] += 1
                    S.op("dve", lambda h: [h.tensor_scalar(out=ringA[i][:, (jj * CW + kk) * 128:(jj * CW + kk + 1) * 128], in0=identb[:],
                                                           scalar1=pcol(V_WDW + cl * CW + kk, pr * 2 + jj), scalar2=None, op0=ALU.mult)
                                           for jj in range(2) for kk in range(CW)],
                         reads=[b_const, b_par], writes=[b_rA[i]])
                    S.op("sp", lambda h: h.dma_start(out=s_diag[cl][:, pr * 2 * CW * 128:(pr + 1) * 2 * CW * 128], in_=ringA[i][:, 0:2 * CW * 128]),
                         reads=[b_rA[i]], writes=[b_sw[("diag", cl)]], dsem=d_diag[cl])
                    if pr == 3:
                        diag_building[0] = False
                for jj in range(2):
                    j = pr * 2 + jj
                    cb = j % 2
                    S.op("pe", lambda h: [h.matmul(tokv(banks[cb][:, :n]), lhsT=ringA[i][:, (jj * CW + kk) * 128:(jj * CW + kk + 1) * 128],
                                                   rhs=ubv[:, j, :, kk:kk + L], start=(kk == 0), stop=(kk == CW - 1)) for kk in range(CW)],
                         reads=[b_rA[i], b_ub[j]], writes=[b_bank[cb]])
                    S.op("act", lambda h: h.activation(out=yT[:, j, :n], in_=banks[cb][:, :n], func=AF.Identity, bias=pcol(V_CBDW + cl, j), scale=1.0),
                         reads=[b_bank[cb], b_par], writes=[b_yT[j]])
                    S.op("act", lambda h: h.activation(out=ysq[j % 2][:, :n], in_=banks[cb][:, :n], func=AF.Square, bias=pcol(V_CBDW + cl, j), scale=1.0),
                         reads=[b_bank[cb], b_par], writes=[b_ysq[j % 2]])
                    if j == 0:
                        S.op("dve", lambda h: h.tensor_copy(out=cst[2][:, :n], in_=yT[:, j, :n]), reads=[b_yT[j]], writes=[b_cst[2]])
                        S.op("dve", lambda h: h.tensor_copy(out=cst[3][:, :n], in_=ysq[j % 2][:, :n]), reads=[b_ysq[j % 2]], writes=[b_cst[3]])
                    else:
                        S.op("dve", lambda h: h.tensor_tensor(out=cst[2][:, :n], in0=cst[2][:, :n], in1=yT[:, j, :n], op=ALU.add),
                             reads=[b_yT[j]], writes=[b_cst[2]])
                        S.op("dve", lambda h: h.tensor_tensor(out=cst[3][:, :n], in0=cst[3][:, :n], in1=ysq[j % 2][:, :n], op=ALU.add),
                             reads=[b_ysq[j % 2]], writes=[b_cst[3]])
            S.op("pe", lambda h: h.matmul(banks[4][:, :n], lhsT=ones[:], rhs=cst[2][:, :n], start=True, stop=True),
                 reads=[b_cst[2], b_const], writes=[b_bank[4]])
            S.op("pe", lambda h: h.matmul(banks[5][:, :n], lhsT=ones[:], rhs=cst[3][:, :n], start=True, stop=True),
                 reads=[b_cst[3], b_const], writes=[b_bank[5]])
            mean, var = cst[0], cst[1]
            S.op("act", lambda h: h.activation(out=mean[:, :n], in_=banks[4][:, :n], func=AF.Identity, scale=1.0 / D, bias=0.0),
                 reads=[b_bank[4]], writes=[b_cst[0]])
            S.op("dve", lambda h: h.tensor_tensor(out=var[:, :n], in0=mean[:, :n], in1=mean[:, :n], op=ALU.mult), reads=[b_cst[0]], writes=[b_cst[1]])
            S.op("dve", lambda h: h.scalar_tensor_tensor(out=var[:, :n], in0=banks[5][:, :n], scalar=1.0 / D, in1=var[:, :n],
                                                         op0=ALU.mult, op1=ALU.subtract), reads=[b_bank[5], b_cst[1]], writes=[b_cst[1]])
            S.op("act", lambda h: h.activation(out=var[:, :n], in_=var[:, :n], func=AF.Sqrt, scale=1.0, bias=EPS), reads=[b_cst[1]], writes=[b_cst[1]])
            S.op("dve", lambda h: h.reciprocal(out=var[:, :n], in_=var[:, :n]), reads=[b_cst[1]], writes=[b_cst[1]])
            for j in range(NCH):
                S.op("dve", lambda h: h.tensor_tensor(out=yT[:, j, :n], in0=yT[:, j, :n], in1=mean[:, :n], op=ALU.subtract),
                     reads=[b_cst[0]], writes=[b_yT[j]])
                S.op("dve", lambda h: h.tensor_tensor(out=yT[:, j, :n], in0=yT[:, j, :n], in1=var[:, :n], op=ALU.mult),
                     reads=[b_cst[1]], writes=[b_yT[j]])
                S.op("act", lambda h: h.activation(out=zT[:, j, :n], in_=yT[:, j, :n], func=AF.Silu, scale=pcol(V_LNG + cl, j), bias=pcol(V_LNB + cl, j)),
                     reads=[b_yT[j], b_par], writes=[b_zT[j]])
            src_co, eng_co, dep_co = wsel(("cout", cl), l, conv_w_out[cl], s_cout[cl])
            wout = src_co.rearrange("(c p) n -> p c n", p=128)
            i = loadA(lambda h, dst: h.dma_start(out=dst[:, :].rearrange("p (c n) -> p c n", c=NCH), in_=wout[:, :, :]), eng_co, dep_co)
            wv = ringA[i][:, :].rearrange("p (c n) -> p c n", c=NCH)
            for dd in range(NCH):
                bk = 6 + dd % 2
                S.op("pe", lambda h: [h.matmul(banks[bk][:, :n], lhsT=wv[:, c, dd * 128:(dd + 1) * 128], rhs=zT[:, c, :n],
                                               start=(c == 0), stop=(c == NCH - 1)) for c in range(NCH)],
                     reads=[b_rA[i]] + b_zT, writes=[b_bank[bk]])
                S.op("dve", lambda h: h.scalar_tensor_tensor(out=xT[:, dd, :n], in0=banks[bk][:, :n], scalar=pcol(V_CBO + cl, dd),
                                                             in1=xT[:, dd, :n], op0=ALU.add, op1=ALU.add),
                     reads=[b_bank[bk], b_par], writes=[b_x[dd]])

        def load_x(n, src_rows):
            switch(io_bufs)
            S.op("pool", lambda h: [h.dma_start(out=xin[0:nt, bi, :], in_=ap) for bi, (ap, nt) in enumerate(src_rows)],
                 writes=[b_xin], dsem=d_xin)
            t0 = 0
            for bi, (ap, nt) in enumerate(src_rows):
                for half in range(2):
                    bk = (bi * 2 + half) % 4
                    S.op("pe", lambda h: [h.transpose(out=banks[bk][:, jj * 128:jj * 128 + nt], in_=xin[0:nt, bi, (half * 4 + jj) * 128:(half * 4 + jj + 1) * 128],
                                                      identity=ident[0:nt, 0:nt]) for jj in range(4)],
                         reads=[b_xin, b_const], writes=[b_bank[bk]])
                    evac(xT[:, half * 4:half * 4 + 4, t0:t0 + nt], banks[bk][:, :].rearrange("p (j t) -> p j t", j=4)[:, :, 0:nt],
                         [b_bank[bk]], b_x[half * 4:half * 4 + 4])
                t0 += nt

        def store_y(n, dst_rows):
            switch(io_bufs)
            for c in range(NCH):
                S.op("act", lambda h: h.activation(out=sq[:, c, :n], in_=xT[:, c, :n], func=AF.Square), reads=[b_x[c]], writes=[b_xin])
            S.op("pe", lambda h: [h.matmul(banks[7][:, :n], lhsT=ones[:], rhs=sq[:, c, :n], start=(c == 0), stop=(c == NCH - 1))
                                  for c in range(NCH)], reads=[b_xin, b_const], writes=[b_bank[7]])
            S.op("act", lambda h: h.activation(out=rs[:, :n], in_=banks[7][:, :n], func=AF.Sqrt, scale=1.0 / D, bias=EPS),
                 reads=[b_bank[7]], writes=[b_rs])
            S.op("dve", lambda h: h.reciprocal(out=rs[:, :n], in_=rs[:, :n]), reads=[b_rs], writes=[b_rs])
            for c in range(NCH):
                S.op("dve", lambda h: h.scalar_tensor_tensor(out=yfin[:, c, :n], in0=xT[:, c, :n], scalar=pcol(V_NF, c),
                                                             in1=rs[:, :n], op0=ALU.mult, op1=ALU.mult),
                     reads=[b_x[c], b_rs, b_par], writes=[b_yfin[c]])
            t0 = 0
            for bi, (ap, nt) in enumerate(dst_rows):
                for half in range(2):
                    bk = (bi * 2 + half) % 4
                    S.op("pe", lambda h: [h.transpose(out=banks[bk][0:nt, jj * 128:(jj + 1) * 128], in_=yfin[:, half * 4 + jj, t0:t0 + nt],
                                                      identity=ident[:]) for jj in range(4)],
                         reads=b_yfin + [b_const], writes=[b_bank[bk]])
                    evac(yout[0:nt, bi, half * 512:(half + 1) * 512], banks[bk][0:nt, :], [b_bank[bk]], [b_yout])
                t0 += nt
            out_dma("y", lambda h: [h.dma_start(out=ap, in_=yout[0:nt, bi, :]) for bi, (ap, nt) in enumerate(dst_rows)], [b_yout])

        first_tile = [True]

        def run_tile(n, prompt, b_idx, t_idx, rows_in, rows_out):
            load_x(n, rows_in)
            for l in range(depth):
                ffn(n, 0, l, V_N1 + l)
                if l == 0 and prompt and cur_tile[0] < depth:
                    emit_precast(cur_tile[0])
                stats = prenorm(n, V_NM + l)
                if l % 2 == 0:
                    attn(n, l, prompt, b_idx, t_idx, stats)
                else:
                    conv(n, l, prompt, b_idx, t_idx, stats)
                ffn(n, 1, l, V_N2 + l)
            store_y(n, rows_out)
            trickle(len(pending))
            cur_tile[0] += 1

        for b_idx in range(2):
            for t_idx in range(NT):
                rin = [(x_prompt[b_idx, t_idx * 512 + k * 128:t_idx * 512 + (k + 1) * 128, :], 128) for k in range(4)]
                rout = [(y_prompt[b_idx, t_idx * 512 + k * 128:t_idx * 512 + (k + 1) * 128, :], 128) for k in range(4)]
                run_tile(512, True, b_idx, t_idx, rin, rout)
        cur_tile[0] = 10 ** 6
        if cfg.get("sample", True):
            rin = [(x_sample.rearrange("b t d -> (b t) d")[:, :], NS)]
            rout = [(y_sample.rearrange("b t d -> (b t) d")[:, :], NS)]
            run_tile(NS, False, 0, 0, rin, rout)

        S.wait_all("pool", out_bufs)
        build.stats = (S.nops, S.nwaits, S.n_dsem)
    return nc


def _pack(inputs, cfg):
    depth = cfg["depth"]
    n_attn = (depth + 1) // 2
    n_conv = depth // 2
    f = lambda k: np.ascontiguousarray(np.asarray(inputs[k], dtype=np.float32))
    nconv1 = max(n_conv, 1)
    cb_in = f("conv_b_in")[:nconv1].reshape(nconv1 * 2, D)
    rows = [f("norm_ffn1")[:depth], f("norm_mix")[:depth], f("norm_ffn2")[:depth], f("norm_final").reshape(1, D),
            cb_in, f("conv_b_dw")[:nconv1], f("conv_ln_g")[:nconv1], f("conv_ln_b")[:nconv1], f("conv_b_out")[:nconv1],
            f("conv_w_dw")[:nconv1].reshape(nconv1 * CW, D)]
    vecs = np.ascontiguousarray(np.concatenate(rows, axis=0).reshape(-1, 128))
    bq = f("attn_b_qkv")[:n_attn]
    qkb = np.ascontiguousarray(bq[:, :1280].reshape(n_attn * 20, 64))
    kvb = np.ascontiguousarray(bq[:, 1024:1536].reshape(n_attn, 1, 512))
    sinks = np.ascontiguousarray(f("attn_sinks")[:n_attn].reshape(n_attn, 1, NH))
    qi = np.arange(128, dtype=np.float32)[:, None]
    si = np.arange(256, dtype=np.float32)[None, :]
    dist = np.abs(np.float32(128.0) + qi - si).astype(np.float32)
    valid = ((qi < 64) & (si < 192)) | ((qi >= 64) & (si >= 64))
    slopes = np.array([2.0 ** (-8.0 * (h + 1) / NH) for h in range(NH)], dtype=np.float32)
    alibi = (-slopes[None, :, None] * dist[:, None, :]).astype(np.float32)
    alibi = np.where(valid[:, None, :], alibi, np.float32(NEG)).astype(np.float32)
    shared = {
        "alibi": np.ascontiguousarray(alibi.reshape(128, NH * 256)),
        "vecs": vecs, "qkb": qkb, "kvb": kvb, "sinks": sinks,
        "ffn1_w_in": f("ffn1_w_in")[:depth], "ffn2_w_in": f("ffn2_w_in")[:depth],
        "ffn1_w_out": f("ffn1_w_out")[:depth], "ffn2_w_out": f("ffn2_w_out")[:depth],
        "attn_w_qkv": f("attn_w_qkv")[:n_attn], "attn_w_o": f("attn_w_o")[:n_attn],
        "conv_w_in": f("conv_w_in")[:nconv1], "conv_w_out": f("conv_w_out")[:nconv1],
    }
    xp, xs = f("x_prompt"), f("x_sample")
    ck = f("cache_k")[:n_attn].reshape(n_attn, -1, WIN, 256)
    cv = f("cache_v")[:n_attn].reshape(n_attn, -1, WIN, 256)
    scv = f("state_conv")[:nconv1]
    maps = []
    for c in range(N_CORES):
        m = dict(shared)
        m["x_prompt"] = np.ascontiguousarray(xp[2 * c:2 * c + 2])
        m["x_sample"] = np.ascontiguousarray(xs[2 * c:2 * c + 2])
        m["cache_k"] = np.ascontiguousarray(ck[:, 2 * c:2 * c + 2])
        m["cache_v"] = np.ascontiguousarray(cv[:, 2 * c:2 * c + 2])
        m["state_conv"] = np.ascontiguousarray(scv[:, 2 * c:2 * c + 2])
        maps.append(m)
    return maps


def run(inputs, cfg):
    nc = build(cfg)
    maps = _pack(inputs, cfg)
    res = run_bass_kernel_spmd(nc, maps, core_ids=list(range(N_CORES)))
    R = res.results
    depth = cfg["depth"]
    n_attn = (depth + 1) // 2
    cat0 = lambda k: np.concatenate([r[k] for r in R], axis=0)
    cat1 = lambda k: np.concatenate([r[k] for r in R], axis=1)
    y_p = cat0("y_prompt")
    y_s = cat0("y_sample")
    kp = cat1("new_k_prompt").reshape(n_attn, -1, WIN, NKV, HD)
    vp = cat1("new_v_prompt").reshape(n_attn, -1, WIN, NKV, HD)
    cp = cat1("new_conv_prompt")
    ks = cat1("new_k_sample").reshape(n_attn, -1, WIN, NKV, HD)
    vs = cat1("new_v_sample").reshape(n_attn, -1, WIN, NKV, HD)
    cs = cat1("new_conv_sample")
    return tuple(np.ascontiguousarray(a, dtype=np.float32) for a in (y_p, y_s, kp, vp, cp, ks, vs, cs))


def kernel(**inputs):
    cfg = {"depth": 4, "seq": 2048, "dseq": 32}
    return run(inputs, cfg)
```
